# Optimizing a Trainium2 kernel written in Bass

```python
import math
import jax, jax.numpy as jnp
from jax import lax
import numpy as np

D_MODEL = 4096
BATCH = 1
SEQ = 8192
DEPTH = 4

GRID_W = 64
CTX_LEN = 256
MIX_WIDTH = D_MODEL
CONV_WIDTH = D_MODEL // 4
SSM_WIDTH = D_MODEL // 4
ATTN_WIDTH = MIX_WIDTH - CONV_WIDTH - SSM_WIDTH
CONV_KERNEL = 31
CONV_HALF = CONV_KERNEL // 2
SSM_CH_PER_GROUP = 16
SSM_GROUPS = SSM_WIDTH // SSM_CH_PER_GROUP
SSM_STATE = 64
V_DIM = 128
N_HEADS = ATTN_WIDTH // V_DIM
QK_DIM = V_DIM // 2
ROPE_AXIS_DIM = QK_DIM // 2
ROPE_BASE = 10000.0
Q_BLOCK = 128
D_FF = 4 * D_MODEL
MOD_RANK = 512
EPS = 1e-6
IN_WIDTH = 2 * CONV_WIDTH + SSM_WIDTH + 3 * ATTN_WIDTH
IN_SPLITS = (2 * CONV_WIDTH,
             2 * CONV_WIDTH + SSM_WIDTH,
             2 * CONV_WIDTH + SSM_WIDTH + ATTN_WIDTH,
             2 * CONV_WIDTH + SSM_WIDTH + 2 * ATTN_WIDTH)

kernel_name = "hymba_style_conv_s5_diffattn_dit"


def _rms_norm(x, g):
    xf = x.astype(jnp.float32)
    y = xf * lax.rsqrt(jnp.mean(xf * xf, axis=-1, keepdims=True) + EPS)
    return (y * g.astype(jnp.float32)).astype(x.dtype)


def _layer_norm(x, g, b):
    xf = x.astype(jnp.float32)
    mu = jnp.mean(xf, axis=-1, keepdims=True)
    var = jnp.mean(jnp.square(xf - mu), axis=-1, keepdims=True)
    y = (xf - mu) * lax.rsqrt(var + EPS)
    return (y * g.astype(jnp.float32) + b.astype(jnp.float32)).astype(x.dtype)


def _modulation(s, wd, wu, b):
    m = (s @ wd) @ wu + b
    return m.reshape(m.shape[0], 1, 6, m.shape[-1] // 6)


def _modulate(h, shift, scale):
    return h * (1 + scale) + shift


def _rope_tables(n_tok):
    rows = n_tok // GRID_W
    row = jnp.repeat(jnp.arange(rows, dtype=jnp.float32), GRID_W)
    col = jnp.tile(jnp.arange(GRID_W, dtype=jnp.float32), rows)
    inv = jnp.power(ROPE_BASE, -jnp.arange(0, ROPE_AXIS_DIM, 2, dtype=jnp.float32) / ROPE_AXIS_DIM)
    ang_r = row[:, None] * inv[None]
    ang_c = col[:, None] * inv[None]
    return jnp.cos(ang_r), jnp.sin(ang_r), jnp.cos(ang_c), jnp.sin(ang_c)


def _rot(v, cos, sin):
    half = v.shape[-1] // 2
    v1, v2 = v[..., :half], v[..., half:]
    return jnp.concatenate([v1 * cos - v2 * sin, v2 * cos + v1 * sin], axis=-1)


def _rope_2d(x, tabs):
    cr, sr, cc, sc = [t[None, :, None, None, :] for t in tabs]
    xf = x.astype(jnp.float32)
    out = jnp.concatenate([_rot(xf[..., :ROPE_AXIS_DIM], cr, sr),
                           _rot(xf[..., ROPE_AXIS_DIM:], cc, sc)], axis=-1)
    return out.astype(x.dtype)


def _conformer_conv(z, dw, dw_b, ln_g, ln_b, pw, pw_b):
    a, g = jnp.split(z, 2, axis=-1)
    u = a * jax.nn.sigmoid(g)
    u = lax.conv_general_dilated(
        u, dw.astype(u.dtype)[:, None, :], window_strides=(1,),
        padding=((CONV_HALF, CONV_HALF),),
        dimension_numbers=("NWC", "WIO", "NWC"),
        feature_group_count=CONV_WIDTH) + dw_b
    u = jax.nn.silu(_layer_norm(u, ln_g, ln_b))
    return u @ pw + pw_b


def _zoh(a_re, a_im, log_step, b_re, b_im):
    a_re = a_re.astype(jnp.float32); a_im = a_im.astype(jnp.float32)
    dt = jnp.exp(log_step.astype(jnp.float32))[:, None]
    mag = jnp.exp(a_re * dt)
    lb_re = mag * jnp.cos(a_im * dt)
    lb_im = mag * jnp.sin(a_im * dt)
    nr, ni = lb_re - 1.0, lb_im
    den = a_re * a_re + a_im * a_im
    f_re = ((nr * a_re + ni * a_im) / den)[..., None]
    f_im = ((ni * a_re - nr * a_im) / den)[..., None]
    b_re = b_re.astype(jnp.float32); b_im = b_im.astype(jnp.float32)
    return lb_re, lb_im, f_re * b_re - f_im * b_im, f_re * b_im + f_im * b_re


def _drive(u, bb_re, bb_im):
    bsz, n = u.shape[0], u.shape[1]
    ug = u.astype(jnp.float32).reshape(bsz, n, SSM_GROUPS, SSM_CH_PER_GROUP)
    return (jnp.einsum("blgh,gph->blgp", ug, bb_re),
            jnp.einsum("blgh,gph->blgp", ug, bb_im))


def _complex_scan(lb_re, lb_im, bu_re, bu_im, h0, reverse):
    a_re = jnp.broadcast_to(lb_re, bu_re.shape)
    a_im = jnp.broadcast_to(lb_im, bu_re.shape)

    def combine(e1, e2):
        a1r, a1i, b1r, b1i = e1
        a2r, a2i, b2r, b2i = e2
        return (a2r * a1r - a2i * a1i, a2r * a1i + a2i * a1r,
                a2r * b1r - a2i * b1i + b2r, a2r * b1i + a2i * b1r + b2i)

    A_re, A_im, s_re, s_im = lax.associative_scan(
        combine, (a_re, a_im, bu_re, bu_im), reverse=reverse, axis=1)
    if h0 is not None:
        h_re, h_im = h0[0][:, None], h0[1][:, None]
        s_re = s_re + A_re * h_re - A_im * h_im
        s_im = s_im + A_re * h_im + A_im * h_re
    return s_re, s_im


def _readout(s_re, s_im, c_re, c_im):
    y = (jnp.einsum("blgp,ghp->blgh", s_re, c_re.astype(jnp.float32))
         - jnp.einsum("blgp,ghp->blgh", s_im, c_im.astype(jnp.float32)))
    return y.reshape(y.shape[0], y.shape[1], SSM_WIDTH)


def _s5_glu(y, w, b):
    g = jax.nn.gelu(y)
    return g * jax.nn.sigmoid(g @ w.astype(jnp.float32) + b.astype(jnp.float32))


def _s5_mixer(u, uc, a_re, a_im, log_step, b_re, b_im, c_re, c_im, d, glu_w, glu_b, with_ctx):
    df = d.astype(jnp.float32)
    y = df * u.astype(jnp.float32)
    yc = df * uc.astype(jnp.float32) if with_ctx else None
    for dr in range(2):
        rev = dr == 1
        lb_re, lb_im, bb_re, bb_im = _zoh(a_re[dr], a_im[dr], log_step[dr], b_re[dr], b_im[dr])
        cb_re, cb_im = _drive(uc, bb_re, bb_im)
        sc_re, sc_im = _complex_scan(lb_re, lb_im, cb_re, cb_im, None, rev)
        end = 0 if rev else -1
        lu_re, lu_im = _drive(u, bb_re, bb_im)
        s_re, s_im = _complex_scan(lb_re, lb_im, lu_re, lu_im,
                                   (sc_re[:, end], sc_im[:, end]), rev)
        y = y + _readout(s_re, s_im, c_re[dr], c_im[dr])
        if with_ctx:
            yc = yc + _readout(sc_re, sc_im, c_re[dr], c_im[dr])
    out = _s5_glu(y, glu_w, glu_b).astype(u.dtype)
    outc = _s5_glu(yc, glu_w, glu_b).astype(uc.dtype) if with_ctx else None
    return out, outc


def _diff_attend(qb, k, v, lam):
    s = jnp.einsum("bqhnd,bkhnd->bhnqk", qb, k,
                   preferred_element_type=jnp.float32) * (QK_DIM ** -0.5)
    p = jax.nn.softmax(s, axis=-1)
    pd = p[:, :, 0] - lam * p[:, :, 1]
    return jnp.einsum("bhqk,bkhe->bqhe", pd.astype(v.dtype), v)


def _diff_attention(zq, zk, zv, zqc, zkc, zvc, lq1, lk1, lq2, lk2, subln_g, lam_init, tabs, with_ctx):
    bsz, n = zq.shape[0], zq.shape[1]
    nc = zkc.shape[1]
    lam = (jnp.exp(jnp.sum(lq1.astype(jnp.float32) * lk1.astype(jnp.float32)))
           - jnp.exp(jnp.sum(lq2.astype(jnp.float32) * lk2.astype(jnp.float32))) + lam_init)
    q = _rope_2d(zq.reshape(bsz, n, N_HEADS, 2, QK_DIM), tabs)
    k = _rope_2d(zk.reshape(bsz, n, N_HEADS, 2, QK_DIM), tabs)
    v = zv.reshape(bsz, n, N_HEADS, V_DIM)
    kc = zkc.reshape(bsz, nc, N_HEADS, 2, QK_DIM)
    vc = zvc.reshape(bsz, nc, N_HEADS, V_DIM)
    k_all = jnp.concatenate([kc, k], axis=1)
    v_all = jnp.concatenate([vc, v], axis=1)
    nb = n // Q_BLOCK
    qblk = jnp.swapaxes(q.reshape(bsz, nb, Q_BLOCK, N_HEADS, 2, QK_DIM), 0, 1)
    o = lax.map(lambda qb: _diff_attend(qb, k_all, v_all, lam), qblk)
    o = jnp.swapaxes(o, 0, 1).reshape(bsz, n, N_HEADS, V_DIM)
    o = (_rms_norm(o, subln_g) * (1.0 - lam_init)).reshape(bsz, n, ATTN_WIDTH)
    oc = None
    if with_ctx:
        qc = zqc.reshape(bsz, nc, N_HEADS, 2, QK_DIM)
        oc = _diff_attend(qc, kc, vc, lam)
        oc = (_rms_norm(oc, subln_g) * (1.0 - lam_init)).reshape(bsz, nc, ATTN_WIDTH)
    return o, oc


def _sq_relu_mlp(h, w1, w2):
    return jnp.square(jax.nn.relu(h @ w1)) @ w2


def setup_inputs(seed: int = 0) -> dict:
    key = jax.random.key(seed)
    ks = iter(jax.random.split(key, 40))

    def nrm(shape, scale):
        return jax.random.normal(next(ks), shape, jnp.float32) * scale

    L, G, P, HG = DEPTH, SSM_GROUPS, SSM_STATE, SSM_CH_PER_GROUP
    n_idx = jnp.arange(P, dtype=jnp.float32)
    a_re = -0.5 + nrm((L, 2, G, P), 0.01)
    a_im = math.pi * n_idx[None, None, None, :] + nrm((L, 2, G, P), 0.01)
    log_step = jax.random.uniform(next(ks), (L, 2, G), jnp.float32,
                                  math.log(1e-3), math.log(1e-1))
    return {
        "x": nrm((BATCH, SEQ, D_MODEL), 1.0),
        "c": nrm((BATCH, D_MODEL), 1.0),
        "ctx": nrm((BATCH, CTX_LEN, D_MODEL), 1.0),
        "c_ctx": nrm((D_MODEL,), 1.0),
        "mod_down": nrm((L, D_MODEL, MOD_RANK), D_MODEL ** -0.5),
        "mod_up": nrm((L, MOD_RANK, 6 * D_MODEL), 0.5 * MOD_RANK ** -0.5),
        "mod_b": nrm((L, 6 * D_MODEL), 0.02),
        "norm_g": 1.0 + nrm((L, 4, D_MODEL), 0.02),
        "w_in": nrm((L, D_MODEL, IN_WIDTH), D_MODEL ** -0.5),
        "conv_dw": nrm((L, CONV_KERNEL, CONV_WIDTH), CONV_KERNEL ** -0.5),
        "conv_dw_b": nrm((L, CONV_WIDTH), 0.02),
        "conv_ln_g": 1.0 + nrm((L, CONV_WIDTH), 0.02),
        "conv_ln_b": nrm((L, CONV_WIDTH), 0.02),
        "conv_pw": nrm((L, CONV_WIDTH, CONV_WIDTH), CONV_WIDTH ** -0.5),
        "conv_pw_b": nrm((L, CONV_WIDTH), 0.02),
        "ssm_a_re": a_re,
        "ssm_a_im": a_im,
        "ssm_log_step": log_step,
        "ssm_b_re": nrm((L, 2, G, P, HG), (2 * HG) ** -0.5),
        "ssm_b_im": nrm((L, 2, G, P, HG), (2 * HG) ** -0.5),
        "ssm_c_re": nrm((L, 2, G, HG, P), (2 * P) ** -0.5),
        "ssm_c_im": nrm((L, 2, G, HG, P), (2 * P) ** -0.5),
        "ssm_d": nrm((L, SSM_WIDTH), 1.0),
        "ssm_glu_w": nrm((L, SSM_WIDTH, SSM_WIDTH), SSM_WIDTH ** -0.5),
        "ssm_glu_b": nrm((L, SSM_WIDTH), 0.02),
        "lam_q1": nrm((L, QK_DIM), 0.1),
        "lam_k1": nrm((L, QK_DIM), 0.1),
        "lam_q2": nrm((L, QK_DIM), 0.1),
        "lam_k2": nrm((L, QK_DIM), 0.1),
        "attn_subln_g": 1.0 + nrm((L, V_DIM), 0.02),
        "w_out": nrm((L, MIX_WIDTH, D_MODEL), MIX_WIDTH ** -0.5),
        "mlp_w1": nrm((L, D_MODEL, D_FF), D_MODEL ** -0.5),
        "mlp_w2": nrm((L, D_FF, D_MODEL), D_FF ** -0.5),
    }


def reference(x, c, ctx, c_ctx, mod_down, mod_up, mod_b, norm_g, w_in,
              conv_dw, conv_dw_b, conv_ln_g, conv_ln_b, conv_pw, conv_pw_b,
              ssm_a_re, ssm_a_im, ssm_log_step, ssm_b_re, ssm_b_im, ssm_c_re, ssm_c_im,
              ssm_d, ssm_glu_w, ssm_glu_b, lam_q1, lam_k1, lam_q2, lam_k2, attn_subln_g,
              w_out, mlp_w1, mlp_w2):
    tabs = _rope_tables(x.shape[1])
    xc = ctx
    s_lat = jax.nn.silu(c)
    s_ctx = jax.nn.silu(c_ctx)[None]
    for l in range(DEPTH):
        with_ctx = l < DEPTH - 1
        lam_init = 0.8 - 0.6 * math.exp(-0.3 * l)
        m = _modulation(s_lat, mod_down[l], mod_up[l], mod_b[l])
        mc = _modulation(s_ctx, mod_down[l], mod_up[l], mod_b[l])

        h = _modulate(_rms_norm(x, norm_g[l, 0]), m[:, :, 0], m[:, :, 1])
        hc = _modulate(_rms_norm(xc, norm_g[l, 0]), mc[:, :, 0], mc[:, :, 1])
        z_conv, z_ssm, z_q, z_k, z_v = jnp.split(h @ w_in[l], IN_SPLITS, axis=-1)
        zc_conv, zc_ssm, zc_q, zc_k, zc_v = jnp.split(hc @ w_in[l], IN_SPLITS, axis=-1)

        y_conv = _conformer_conv(z_conv, conv_dw[l], conv_dw_b[l], conv_ln_g[l], conv_ln_b[l],
                                 conv_pw[l], conv_pw_b[l])
        y_ssm, yc_ssm = _s5_mixer(z_ssm, zc_ssm, ssm_a_re[l], ssm_a_im[l], ssm_log_step[l],
                                  ssm_b_re[l], ssm_b_im[l], ssm_c_re[l], ssm_c_im[l],
                                  ssm_d[l], ssm_glu_w[l], ssm_glu_b[l], with_ctx)
        y_att, yc_att = _diff_attention(z_q, z_k, z_v, zc_q, zc_k, zc_v,
                                        lam_q1[l], lam_k1[l], lam_q2[l], lam_k2[l],
                                        attn_subln_g[l], lam_init, tabs, with_ctx)
        y = jnp.concatenate([y_conv, y_ssm, y_att], axis=-1) @ w_out[l]
        x = x + m[:, :, 2] * _rms_norm(y, norm_g[l, 1])
        if with_ctx:
            yc_conv = _conformer_conv(zc_conv, conv_dw[l], conv_dw_b[l], conv_ln_g[l],
                                      conv_ln_b[l], conv_pw[l], conv_pw_b[l])
            yc = jnp.concatenate([yc_conv, yc_ssm, yc_att], axis=-1) @ w_out[l]
            xc = xc + mc[:, :, 2] * _rms_norm(yc, norm_g[l, 1])

        h = _modulate(_rms_norm(x, norm_g[l, 2]), m[:, :, 3], m[:, :, 4])
        x = x + m[:, :, 5] * _rms_norm(_sq_relu_mlp(h, mlp_w1[l], mlp_w2[l]), norm_g[l, 3])
        if with_ctx:
            hc = _modulate(_rms_norm(xc, norm_g[l, 2]), mc[:, :, 3], mc[:, :, 4])
            xc = xc + mc[:, :, 5] * _rms_norm(_sq_relu_mlp(hc, mlp_w1[l], mlp_w2[l]),
                                              norm_g[l, 3])
    return x
```

```python
import math
from contextlib import ExitStack

import numpy as np
import concourse.bass as bass
import concourse.mybir as mybir
from concourse.bass_utils import run_bass_kernel_spmd

F32 = mybir.dt.float32
BF16 = mybir.dt.bfloat16
I32 = mybir.dt.int32
AF = mybir.ActivationFunctionType
ALU = mybir.AluOpType

COMPUTE = ('pe', 'act', 'dve', 'pool')
NDMASEM = 8
TWO_PI = 2.0 * math.pi


class Ev:
    __slots__ = ('sem', 'val')

    def __init__(self, sem, val):
        self.sem = sem
        self.val = val


class Sched:
    def __init__(self, nc, dma_queues=('sp', 'pool', 'act')):
        self.nc = nc
        self.streams = {e: [] for e in ('pe', 'act', 'dve', 'pool', 'sp')}
        self.cnt = {e: 0 for e in COMPUTE}
        self.semnames = ['c_' + e for e in COMPUTE]
        self.dq = {}
        for q in dma_queues:
            self.dq[q] = dict(i=0, cnt=[0] * NDMASEM)
            for k in range(NDMASEM):
                self.semnames.append('d_%s_%d' % (q, k))
        self.known = {e: {} for e in self.streams}
        self.track = {}
        self.rr = 0

    def _deps(self, reads, writes):
        evs = []
        for k in reads:
            t = self.track.get(k)
            if t and t['w'] is not None:
                evs.append(t['w'])
        for k in writes:
            t = self.track.get(k)
            if t:
                if t['w'] is not None:
                    evs.append(t['w'])
                evs.extend(t['r'])
        return evs

    def _commit(self, ev, reads, writes):
        for k in reads:
            t = self.track.setdefault(k, {'w': None, 'r': []})
            t['r'].append(ev)
            if len(t['r']) > 64:
                best = {}
                for e in t['r']:
                    if e.val > best.get(e.sem, Ev(e.sem, 0)).val:
                        best[e.sem] = e
                t['r'] = list(best.values())
        for k in writes:
            self.track[k] = {'w': ev, 'r': []}

    def _emit_waits(self, eng, evs):
        need = {}
        for ev in evs:
            if ev.val > need.get(ev.sem, 0):
                need[ev.sem] = ev.val
        kn = self.known[eng]
        for sname, v in need.items():
            if kn.get(sname, 0) >= v:
                continue
            kn[sname] = v
            self.streams[eng].append(('wait', sname, v))

    def op(self, eng, fn, reads=(), writes=()):
        reads = [k for k in reads if k is not None]
        writes = [k for k in writes if k is not None]
        evs = self._deps(reads, writes)
        if eng == 'pe':
            evs = [e for e in evs if e.sem != 'c_pe']
        self._emit_waits(eng, evs)
        self.cnt[eng] += 1
        ev = Ev('c_' + eng, self.cnt[eng])
        self.streams[eng].append(('op', fn, ev.sem))
        self._commit(ev, reads, writes)
        return ev

    def i(self, eng, meth, *args, reads=(), writes=(), **kw):
        return self.op(eng, lambda e: getattr(e, meth)(*args, **kw), reads, writes)

    def dma(self, q, out, in_, reads=(), writes=(), **kw):
        reads = [k for k in reads if k is not None]
        writes = [k for k in writes if k is not None]
        st = self.dq[q]
        slot = st['i'] % NDMASEM
        st['i'] += 1
        sem = 'd_%s_%d' % (q, slot)
        evs = self._deps(reads, writes)
        if st['cnt'][slot] > 0:
            evs.append(Ev(sem, st['cnt'][slot]))
        self._emit_waits(q, evs)
        st['cnt'][slot] += 16
        ev = Ev(sem, st['cnt'][slot])
        self.streams[q].append(('dma', out, in_, sem, kw))
        self._commit(ev, reads, writes)
        return ev

    def ld(self, out, in_, reads=(), writes=(), **kw):
        self.rr += 1
        return self.dma('sp' if self.rr % 2 else 'pool', out, in_, reads, writes, **kw)

    def barrier(self):
        evs = [Ev('c_' + e, self.cnt[e]) for e in COMPUTE if self.cnt[e] > 0]
        for q, st in self.dq.items():
            for k in range(NDMASEM):
                if st['cnt'][k] > 0:
                    evs.append(Ev('d_%s_%d' % (q, k), st['cnt'][k]))
        for eng in self.streams:
            self._emit_waits(eng, evs)
        self.track = {}

    def emit(self):
        nc = self.nc
        with ExitStack() as es:
            sems = {n: es.enter_context(nc.semaphore(n)) for n in self.semnames}
            block = es.enter_context(nc.Block())
            streams = self.streams

            def run(engobj, name):
                for it in streams[name]:
                    if it[0] == 'wait':
                        engobj.wait_ge(sems[it[1]], it[2])
                    elif it[0] == 'op':
                        it[1](engobj).then_inc(sems[it[2]], 1)
                    else:
                        engobj.dma_start(out=it[1], in_=it[2], **it[4]).then_inc(sems[it[3]], 16)

            @block.tensor
            def _(e):
                run(e, 'pe')

            @block.scalar
            def _(e):
                run(e, 'act')

            @block.vector
            def _(e):
                run(e, 'dve')

            @block.gpsimd
            def _(e):
                run(e, 'pool')

            @block.sync
            def _(e):
                run(e, 'sp')


FULL = dict(D=4096, NL=8192, NCX=256, L=4, R=512, GW=64)
CK = 31
EPS = 1e-6


def build(cfg, debug=False):
    D = cfg['D']; NL = cfg['NL']; NCX = cfg['NCX']; L = cfg['L']; R = cfg['R']
    DC = D // 128; N = NL + NCX; TB = 512
    CW = D // 4; CC = CW // 128; SW = D // 4; SC = SW // 128; AW = D // 2; H = AW // 128
    FF = 4 * D; RC = R // 128; INW = 2 * CW + SW + 3 * AW; G = SW // 16
    MC = D // 128
    blocks = [(0, NCX)] + [(NCX + i * TB, TB) for i in range(NL // TB)]
    NBLK = len(blocks)
    NKT = N // 128
    QF = min(FF, cfg.get('QF', 4096)); NQ = FF // QF; QFC = QF // 128

    nc = bass.Bass("TRN2", target_bir_lowering=False)
    s = Sched(nc)
    es = ExitStack()

    def din(name, shape, dt=F32):
        return nc.dram_tensor(name, list(shape), dt, kind="ExternalInput").ap()

    def dscr(name, shape, dt=F32, dbg=False):
        return nc.dram_tensor(name, list(shape), dt, kind="ExternalOutput" if (dbg and debug) else "Internal").ap()

    xT_in = din("xT", [DC, 128, N])
    cT_in = din("cT", [128, DC, 2])
    mod_down = din("mod_down", [L, D, R]); mod_up = din("mod_up", [L, R, 6 * D])
    mod_bT = din("mod_bT", [L, 128, 6 * DC]); norm_gT = din("norm_gT", [L, 128, 4 * DC])
    w_in = din("w_in", [L, D, INW]); conv_pw = din("conv_pw", [L, CW, CW]); glu_w = din("ssm_glu_w", [L, SW, SW])
    w_out = din("w_out", [L, D, D]); w1 = din("mlp_w1", [L, D, FF]); w2 = din("mlp_w2", [L, FF, D])
    conv_dwT = din("conv_dwT", [L, 128, CC * CK]); convv = din("convv", [L, 128, 4 * CC])
    areF = din("areF", [L, 2, 128, SC * 64]); aimF = din("aimF", [L, 2, 128, SC * 64]); lsF = din("lsF", [L, 2, 128, SC * 64])
    breF = din("breF", [L, 2, 128, SC * 64]); bimF = din("bimF", [L, 2, 128, SC * 64])
    cS = din("cS", [L, 2, 128, SC * 128])
    areS = din("areS", [L, 2, 128, G]); aimS = din("aimS", [L, 2, 128, G]); lsS = din("lsS", [L, 2, 128, G])
    ssmv = din("ssmv", [L, 128, 2 * SC])
    lamv = din("lamv", [L, 4, 64]); sublnT = din("sublnT", [L, 128, 1])
    posT = din("posT", [128, N]); jfreq = din("jfreq", [128, 1])
    rowmask_in = din("rowmask", [128, 8]); iota_in = din("iota512", [128, 512]); ropeR_in = din("ropeR", [128, 128])
    laminit_in = din("laminit", [128, 2])

    XT = nc.dram_tensor("XT", [DC, 128, N], F32, kind="ExternalOutput").ap()
    UT = dscr("UT", [CC, 128, N], dbg=True)
    ZS = dscr("ZS", [SC, 128, N], dbg=True)
    YS = dscr("YS", [SC, 128, N], dbg=True)
    QT = dscr("QT", [H, 128, N], BF16); KT = dscr("KT", [H, 128, N], BF16)
    V = dscr("V", [N, AW], BF16)
    YT = dscr("YT", [MC, 128, N], BF16, dbg=True)
    wb_in = dscr("wb_in", [D, INW], BF16); wb_pw = dscr("wb_pw", [CW, CW], BF16); wb_glu = dscr("wb_glu", [SW, SW], BF16)
    wb_out = dscr("wb_out", [D, D], BF16); wb_1 = dscr("wb_1", [D, FF], BF16); wb_2 = dscr("wb_2", [FF, D], BF16)

    def SB(name, shape, dt=F32):
        return es.enter_context(nc.sbuf_tensor(name, list(shape), dt))

    ps = [es.enter_context(nc.psum_tensor("ps%d" % i, [128, 512], F32)) for i in range(8)]

    ones_bf = SB("ones_bf", [128, 128], BF16); ones_f = SB("ones_f", [128, 128])
    iota = SB("iota", [128, 512]); rowmask = SB("rowmask_sb", [128, 8]); ropeR = SB("ropeR_sb", [128, 128])
    sT = SB("sT", [128, DC, 2]); mT = SB("mT", [128, 6 * DC, 2]); mb = SB("mb", [128, 6 * DC])
    ng = SB("ng", [128, 4 * DC])
    gs1 = SB("gs1", [128, DC, 2]); gs2 = SB("gs2", [128, DC, 2]); gg1 = SB("gg1", [128, DC, 2]); gg2 = SB("gg2", [128, DC, 2])
    tT = SB("tT", [128, RC, 2])
    lamt = SB("lamt", [128, 4, 64]); lam = SB("lam", [128, 1]); nlam = SB("nlam", [128, 1]); subg = SB("subg", [128, 1])
    lsc = SB("lsc", [128, 4]); laminit = SB("laminit_sb", [128, 2])
    wk = [SB("wk%d" % i, [128, 512]) for i in range(8)]
    wki = SB("wki", [128, 512], I32)
    wkb = [SB("wkb%d" % i, [128, 512], BF16) for i in range(6)]
    rstd = SB("rstd", [128, 512])
    KCM = max(DC, QFC)
    YB = max(DC * 2048, 65536); HB = max((DC + 1) * 1024, ((N * 4 + 63) // 64) * 64); AB = max(KCM * 1024, ((N * 2 + 63) // 64) * 64); SLB = KCM * 512
    R_Y = 0; R_H = YB; R_A = R_H + HB; R_S = R_A + AB
    arena = SB("arena", [128, (YB + HB + AB + 2 * SLB) // 4])

    def view(off, shape, dt=F32):
        n = 1
        for d in shape:
            n *= d
        bpe = 4 if dt in (F32, I32) else 2
        assert off % 4 == 0
        v = arena[:, off // 4: off // 4 + (n * bpe + 3) // 4]
        if dt != F32:
            v = v.bitcast(dt)
        v = v[:, 0:n]
        if len(shape) == 2:
            return v.rearrange("p (a b) -> p a b", b=shape[1])
        if len(shape) == 3:
            return v.rearrange("p (a b c) -> p a b c", b=shape[1], c=shape[2])
        if len(shape) == 4:
            return v.rearrange("p (a b c d) -> p a b c d", b=shape[1], c=shape[2], d=shape[3])
        return v

    class Bump:
        def __init__(self, off, size):
            self.off = off; self.end = off + size

        def take(self, shape, dt=F32):
            n = 1
            for d in shape:
                n *= d
            nb = ((n * (4 if dt in (F32, I32) else 2) + 63) // 64) * 64
            v = view(self.off, shape, dt)
            self.off += nb
            assert self.off <= self.end, (self.off, self.end)
            return v

    yacc = view(R_Y, [DC, 512])
    hT = view(R_H, [DC + 1, 512], BF16)
    aT = view(R_A, [KCM, 512], BF16)
    slab = [view(R_S + i * SLB, [KCM, 256], BF16) for i in range(2)]
    cst = [view(R_Y + i * 8192, [2048]) for i in range(2)]
    cstb = [view(R_Y + 16384 + i * 4096, [2048], BF16) for i in range(2)]
    ubuf = [view(R_H + i * 2304, [544]) for i in range(2)]
    zsn = view(R_Y, [N])
    ysum = view(R_H, [N])
    bY = Bump(R_Y + ((N * 4 + 63) // 64) * 64, YB - ((N * 4 + 63) // 64) * 64)
    sF = [bY.take([SC * 64]) for i in range(10)]
    sFi = bY.take([SC * 64], I32)
    bA = Bump(R_A, AB + 2 * SLB)
    ssm_r = bA.take([2, G]); ssm_phi = bA.take([2, G]); ssm_phi64 = bA.take([2, G])
    ssm_off = bA.take([2, G, NBLK])
    Bd1 = bA.take([2, SC, 128]); Bd2 = bA.take([2, SC, 128]); CmD = bA.take([2, SC, 128])
    lhsB = bA.take([2 * 2 * 8, 128], BF16)
    lhsC = bA.take([2 * 8, 128], BF16)
    carry = bA.take([2 * 2, 8])
    sG = [bA.take([G]) for i in range(4)]
    sGi = bA.take([G], I32)
    kth = view(R_H, [N], BF16)
    vth = view(R_A, [NKT * 128], BF16)
    cosb = SB("cosb", [128, 512]); sinb = SB("sinb", [128, 512])
    COSD = dscr("COSD", [128, N]); SIND = dscr("SIND", [128, N])

    cnt = {'cast': 0, 'slab': 0, 'wk': 0}

    def frac(dst, src, key_dst, key_src, T, eng='dve'):
        s.i(eng, 'tensor_copy', wki[:, 0:T], src, reads=[key_src], writes=['wki'])
        s.i(eng, 'tensor_copy', wk[7][:, 0:T], wki[:, 0:T], reads=['wki'], writes=['wk7'])
        s.i(eng, 'tensor_tensor', dst, src, wk[7][:, 0:T], ALU.subtract, reads=[key_src, 'wk7'], writes=[key_dst])

    def sincos(sin_dst, cos_dst, r, ksin, kcos, kr, T, tmp, ktmp):
        s.i('act', 'activation', out=sin_dst, in_=r, func=AF.Sin, scale=TWO_PI, reads=[kr], writes=[ksin])
        s.i('dve', 'scalar_tensor_tensor', tmp, r, -1.0, r, ALU.mult, ALU.max, reads=[kr], writes=[ktmp])
        s.i('act', 'activation', out=cos_dst, in_=tmp, func=AF.Sin, scale=-TWO_PI, bias=halfpi[:, 0:1],
            reads=[ktmp], writes=[kcos])

    def cast_weight(src, dst, K, M):
        cw = [c for c in (2048, 1536, 1024, 512, 256, 128) if M % c == 0][0]
        for rc in range(K // 128):
            for c0 in range(0, M, cw):
                i = cnt['cast']; cnt['cast'] += 1
                b = i % 2
                s.ld(cst[b][:, 0:cw], src[rc * 128:(rc + 1) * 128, c0:c0 + cw], writes=['cst%d' % b])
                eng = ('act', 'pool', 'dve')[i % 3]
                if eng == 'act':
                    s.i('act', 'activation', out=cstb[b][:, 0:cw], in_=cst[b][:, 0:cw], func=AF.Copy,
                        reads=['cst%d' % b], writes=['cstb%d' % b])
                else:
                    s.i(eng, 'tensor_copy', cstb[b][:, 0:cw], cst[b][:, 0:cw], reads=['cst%d' % b], writes=['cstb%d' % b])
                s.ld(dst[rc * 128:(rc + 1) * 128, c0:c0 + cw], cstb[b][:, 0:cw], reads=['cstb%d' % b], writes=[])

    def load_slab(wb, r0, KCn, c0, cols):
        i = cnt['slab']; cnt['slab'] += 1
        b = i % 2
        for k0 in range(0, KCn, 8):
            k1 = min(KCn, k0 + 8)
            s.ld(slab[b][:, k0:k1, 0:cols],
                 wb[r0 + k0 * 128:r0 + k1 * 128, c0:c0 + cols].rearrange("(kc p) c -> p kc c", p=128),
                 writes=['slab%d' % b])
        return slab[b], 'slab%d' % b

    psrr = {'i': 0}

    def next_ps(lo=0, hi=8):
        i = lo + psrr['i'] % (hi - lo)
        psrr['i'] += 1
        return ps[i], 'ps%d' % i

    def stats_rstd(src_chunks, nch, T, denom):
        p, pk = ps[7], 'ps7'
        for c in range(nch):
            ap, k = src_chunks(c)
            b = c % 2
            s.i('act', 'activation', out=wkb[b][:, 0:T], in_=ap, func=AF.Square, reads=[k], writes=['wkb%d' % b])
            s.i('pe', 'matmul', p[:, 0:T], ones_bf[:], wkb[b][:, 0:T], start=(c == 0), stop=(c == nch - 1),
                reads=['wkb%d' % b, 'ones'], writes=[pk])
        s.i('act', 'activation', out=rstd[:, 0:T], in_=p[:, 0:T], func=AF.Sqrt, scale=1.0 / denom, bias=epsc[:, 0:1],
            reads=[pk], writes=['rstd'])
        s.i('dve', 'reciprocal', rstd[:, 0:T], rstd[:, 0:T], reads=['rstd'], writes=['rstd'])

    halfpi = SB("halfpi", [128, 1]); epsc = SB("epsc", [128, 1]); zero_c = SB("zero_c", [128, 1])
    s.i('pool', 'memset', halfpi[:], math.pi / 2, writes=['halfpi'])
    s.i('pool', 'memset', epsc[:], EPS, writes=['epsc'])
    s.i('pool', 'memset', zero_c[:], 0.0, writes=['zero_c'])
    s.i('pool', 'memset', ones_bf[:], 1.0, writes=['ones'])
    s.i('pool', 'memset', ones_f[:], 1.0, writes=['ones_f'])
    s.ld(iota[:], iota_in, writes=['iota'])
    s.ld(rowmask[:], rowmask_in, writes=['rowmask'])
    s.ld(ropeR[:], ropeR_in, writes=['ropeR'])
    s.ld(laminit[:], laminit_in, writes=['laminit'])
    for dc in range(DC):
        for t0 in range(0, N, 2048):
            t1 = min(N, t0 + 2048)
            b = cnt['cast'] % 2; cnt['cast'] += 1
            s.ld(cst[b][:, 0:t1 - t0], xT_in[dc, :, t0:t1], writes=['cst%d' % b])
            s.ld(XT[dc, :, t0:t1], cst[b][:, 0:t1 - t0], reads=['cst%d' % b])
    jf = SB("jf", [128, 1])
    s.ld(jf[:], jfreq, writes=['jf'])
    s.i('act', 'activation', out=jf[:], in_=jf[:], func=AF.Exp, scale=-math.log(10000.0) / 16.0, reads=['jf'], writes=['jf'])
    s.i('dve', 'tensor_scalar', jf[:], jf[:], 1.0 / TWO_PI, None, ALU.mult, reads=['jf'], writes=['jf'])
    for t0 in range(0, N, 512):
        T = min(512, N - t0)
        s.ld(wk[0][:, 0:T], posT[:, t0:t0 + T], writes=['wk0'])
        s.i('dve', 'tensor_scalar', wk[1][:, 0:T], wk[0][:, 0:T], jf[:, 0:1], None, ALU.mult, reads=['wk0', 'jf'], writes=['wk1'])
        frac(wk[2][:, 0:T], wk[1][:, 0:T], 'wk2', 'wk1', T)
        sincos(sinb[:, 0:T], cosb[:, 0:T], wk[2][:, 0:T], 'sinb', 'cosb', 'wk2', T, wk[3][:, 0:T], 'wk3')
        s.ld(SIND[:, t0:t0 + T], sinb[:, 0:T], reads=['sinb'])
        s.ld(COSD[:, t0:t0 + T], cosb[:, 0:T], reads=['cosb'])
    s.ld(sT[:], cT_in, writes=['sT'])
    s.i('act', 'activation', out=sT[:], in_=sT[:], func=AF.Silu, reads=['sT'], writes=['sT'])

    dwv = SB("dwv", [128, CC * CK]); cvv = SB("cvv", [128, 4 * CC]); smv = SB("smv", [128, 2 * SC])

    for l in range(L):
        cast_weight(w_in[l], wb_in, D, INW)
        cast_weight(conv_pw[l], wb_pw, CW, CW)
        cast_weight(glu_w[l], wb_glu, SW, SW)
        cast_weight(w_out[l], wb_out, D, D)
        cast_weight(w1[l], wb_1, D, FF)
        cast_weight(w2[l], wb_2, FF, D)
        s.ld(mb[:], mod_bT[l], writes=['mb'])
        s.ld(ng[:], norm_gT[l], writes=['ng'])
        for rc in range(RC):
            for d0 in range(0, DC, 8):
                dn = min(8, DC - d0)
                b = cnt['cast'] % 2; cnt['cast'] += 1
                s.ld(cst[b][:, 0:dn * 128].rearrange("p (a c) -> p a c", c=128),
                     mod_down[l, d0 * 128:(d0 + dn) * 128, rc * 128:(rc + 1) * 128].rearrange("(a p) c -> p a c", p=128),
                     writes=['cst%d' % b])
                for a in range(dn):
                    dc = d0 + a
                    s.i('pe', 'matmul', ps[0][:, 0:2], cst[b][:, a * 128:(a + 1) * 128], sT[:, dc, :],
                        start=(dc == 0), stop=(dc == DC - 1), reads=['cst%d' % b, 'sT'], writes=['ps0'])
            s.i('act', 'activation', out=tT[:, rc, :], in_=ps[0][:, 0:2], func=AF.Copy, reads=['ps0'], writes=['tT'])
        for m0 in range(0, 6 * DC, 4):
            b = cnt['cast'] % 2; cnt['cast'] += 1
            s.ld(cst[b][:, 0:RC * 512].rearrange("p (a c) -> p a c", c=512),
                 mod_up[l, :, m0 * 128:(m0 + 4) * 128].rearrange("(a p) c -> p a c", p=128), writes=['cst%d' % b])
            for mm in range(4):
                p, pk = next_ps(0, 4)
                for rc in range(RC):
                    s.i('pe', 'matmul', p[:, 0:2], cst[b][:, rc * 512 + mm * 128: rc * 512 + (mm + 1) * 128], tT[:, rc, :],
                        start=(rc == 0), stop=(rc == RC - 1), reads=['cst%d' % b, 'tT'], writes=[pk])
                s.i('dve', 'tensor_scalar', mT[:, m0 + mm, :], p[:, 0:2], mb[:, m0 + mm:m0 + mm + 1], None, ALU.add,
                    reads=[pk, 'mb'], writes=['mT'])
        for dc in range(DC):
            for (gsx, gix, scx) in ((gs1, 0, 1), (gs2, 2, 4)):
                s.i('dve', 'tensor_scalar', gsx[:, dc, :], mT[:, scx * DC + dc, :], 1.0, ng[:, gix * DC + dc:gix * DC + dc + 1],
                    ALU.add, ALU.mult, reads=['mT', 'ng'], writes=['gsx'])
            for (ggx, gix, gtx) in ((gg1, 1, 2), (gg2, 3, 5)):
                s.i('dve', 'tensor_scalar', ggx[:, dc, :], mT[:, gtx * DC + dc, :], ng[:, gix * DC + dc:gix * DC + dc + 1], None,
                    ALU.mult, reads=['mT', 'ng'], writes=['gsx'])
        s.ld(lamt[:].rearrange("p a b -> p (a b)"), lamv[l:l + 1].rearrange("o a b -> o (a b)").partition_broadcast(128), writes=['lamt'])
        s.ld(subg[:], sublnT[l], writes=['subg'])
        s.i('dve', 'tensor_tensor', wk[0][:, 0:64], lamt[:, 0, :], lamt[:, 1, :], ALU.mult, reads=['lamt'], writes=['wk0'])
        s.i('dve', 'tensor_tensor', wk[0][:, 64:128], lamt[:, 2, :], lamt[:, 3, :], ALU.mult, reads=['lamt'], writes=['wk0'])
        s.i('dve', 'reduce_sum', lsc[:, 0:1], wk[0][:, 0:64], mybir.AxisListType.X, reads=['wk0'], writes=['lsc'])
        s.i('dve', 'reduce_sum', lsc[:, 1:2], wk[0][:, 64:128], mybir.AxisListType.X, reads=['wk0'], writes=['lsc'])
        s.i('act', 'activation', out=lsc[:, 2:4], in_=lsc[:, 0:2], func=AF.Exp, reads=['lsc'], writes=['lsc'])
        s.i('dve', 'tensor_tensor', lam[:], lsc[:, 2:3], lsc[:, 3:4], ALU.subtract, reads=['lsc'], writes=['lam'])
        s.i('dve', 'tensor_scalar', lam[:], lam[:], laminit[:, 0:1], None, ALU.add, reads=['lam', 'laminit'], writes=['lam'])
        s.i('dve', 'tensor_scalar', nlam[:], lam[:], -1.0, None, ALU.mult, reads=['lam'], writes=['nlam'])
        s.i('dve', 'tensor_scalar', subg[:], subg[:], laminit[:, 1:2], None, ALU.mult, reads=['subg', 'laminit'], writes=['subg'])
        s.ld(dwv[:], conv_dwT[l], writes=['dwv']); s.ld(cvv[:], convv[l], writes=['cvv']); s.ld(smv[:], ssmv[l], writes=['smv'])

        s.barrier()

        for bi, (t0, T) in enumerate(blocks):
            w = 1 if bi == 0 else 0
            s.ld(cosb[:, 0:T], COSD[:, t0:t0 + T], writes=['cosb']); s.ld(sinb[:, 0:T], SIND[:, t0:t0 + T], writes=['sinb'])
            for dc in range(DC):
                s.ld(yacc[:, dc, 0:T], XT[dc, :, t0:t0 + T], writes=['yacc%d' % dc])
            stats_rstd(lambda c: (yacc[:, c, 0:T], 'yacc%d' % c), DC, T, float(D))
            for dc in range(DC):
                b = dc % 2
                s.i('dve', 'tensor_tensor', wk[b][:, 0:T], yacc[:, dc, 0:T], rstd[:, 0:T], ALU.mult,
                    reads=['yacc%d' % dc, 'rstd'], writes=['wk%d' % b])
                s.i('act', 'activation', out=hT[:, dc, 0:T], in_=wk[b][:, 0:T], func=AF.Identity,
                    scale=gs1[:, dc, w:w + 1], bias=mT[:, 0 * DC + dc, w:w + 1], reads=['wk%d' % b, 'gsx', 'mT'], writes=['hT'])
            NOC_FM = (2 * CW + SW + 2 * AW) // 128
            order = []
            for j in range(CC):
                order += [('a', j, j), ('g', j, CC + j)]
            for j in range(SC):
                order.append(('s', j, 2 * CC + j))
            for hh in range(H):
                order.append(('q', hh, 2 * CC + SC + hh))
            for hh in range(H):
                order.append(('k', hh, 2 * CC + SC + H + hh))
            for (kind, j, oc) in order:
                sl, sk = load_slab(wb_in, 0, DC, oc * 128, 128)
                p, pk = next_ps(0, 4)
                for dc in range(DC):
                    s.i('pe', 'matmul', p[:, 0:T], sl[:, dc, 0:128], hT[:, dc, 0:T], start=(dc == 0), stop=(dc == DC - 1),
                        reads=[sk, 'hT'], writes=[pk])
                if kind == 'a':
                    s.i('act', 'activation', out=wk[2][:, 0:T], in_=p[:, 0:T], func=AF.Copy, reads=[pk], writes=['wk2'])
                elif kind == 'g':
                    s.i('act', 'activation', out=wk[3][:, 0:T], in_=p[:, 0:T], func=AF.Sigmoid, reads=[pk], writes=['wk3'])
                    s.i('dve', 'tensor_tensor', wk[4][:, 0:T], wk[2][:, 0:T], wk[3][:, 0:T], ALU.mult, reads=['wk2', 'wk3'], writes=['wk4'])
                    s.ld(UT[j, :, t0:t0 + T], wk[4][:, 0:T], reads=['wk4'])
                elif kind == 's':
                    s.i('act', 'activation', out=wk[5][:, 0:T], in_=p[:, 0:T], func=AF.Copy, reads=[pk], writes=['wk5'])
                    s.ld(ZS[j, :, t0:t0 + T], wk[5][:, 0:T], reads=['wk5'])
                else:
                    s.i('act', 'activation', out=wk[2][:, 0:T], in_=p[:, 0:T], func=AF.Copy, reads=[pk], writes=['wk2'])
                    p2, pk2 = next_ps(4, 6)
                    s.i('pe', 'matmul', p2[:, 0:T], ropeR[:], wk[2][:, 0:T], start=True, stop=True, reads=['ropeR', 'wk2'], writes=[pk2])
                    s.i('dve', 'tensor_tensor', wk[3][:, 0:T], wk[2][:, 0:T], cosb[:, 0:T], ALU.mult, reads=['wk2', 'cosb'], writes=['wk3'])
                    s.i('dve', 'tensor_tensor', wk[4][:, 0:T], p2[:, 0:T], sinb[:, 0:T], ALU.mult, reads=[pk2, 'sinb'], writes=['wk4'])
                    b = cnt['wk'] % 2; cnt['wk'] += 1
                    s.i('dve', 'tensor_tensor', wkb[2 + b][:, 0:T], wk[3][:, 0:T], wk[4][:, 0:T], ALU.add, reads=['wk3', 'wk4'], writes=['wkb%d' % (2 + b)])
                    s.ld((QT if kind == 'q' else KT)[j, :, t0:t0 + T], wkb[2 + b][:, 0:T], reads=['wkb%d' % (2 + b)])
            vc0 = 2 * CW + SW + 2 * AW
            for c0 in range(0, AW, 256):
                sl, sk = load_slab(wb_in, 0, DC, vc0 + c0, 256)
                for tt in range(T // 128):
                    p, pk = next_ps(0, 4)
                    for dc in range(DC):
                        s.i('pe', 'matmul', p[:, 0:256], hT[:, dc, tt * 128:(tt + 1) * 128], sl[:, dc, 0:256],
                            start=(dc == 0), stop=(dc == DC - 1), reads=[sk, 'hT'], writes=[pk])
                    b = cnt['wk'] % 2; cnt['wk'] += 1
                    s.i('act', 'activation', out=wkb[2 + b][:, 0:256], in_=p[:, 0:256], func=AF.Copy, reads=[pk], writes=['wkb%d' % (2 + b)])
                    s.ld(V[t0 + tt * 128:t0 + (tt + 1) * 128, c0:c0 + 256], wkb[2 + b][:, 0:256], reads=['wkb%d' % (2 + b)])
        s.barrier()

        for bi, (t0, T) in enumerate(blocks):
            seg_lo, seg_hi = (0, NCX) if bi == 0 else (NCX, N)
            lo = max(seg_lo, t0 - 15); hi = min(seg_hi, t0 + T + 15)
            for j in range(CC):
                ub = ubuf[j % 2]; ubk = 'ub%d' % (j % 2)
                if lo > t0 - 15 or True:
                    s.i('pool', 'memset', ub[:, 0:T + 30], 0.0, writes=[ubk])
                s.ld(ub[:, lo - (t0 - 15): hi - (t0 - 15)], UT[j, :, lo:hi], writes=[ubk])
                acc = yacc[:, j, 0:T]; ak = 'yacc%d' % j
                s.i('dve', 'tensor_scalar', acc, ub[:, 0:T], dwv[:, j * CK:j * CK + 1], cvv[:, 0 * CC + j:0 * CC + j + 1],
                    ALU.mult, ALU.add, reads=[ubk, 'dwv', 'cvv'], writes=[ak])
                for k in range(1, CK):
                    s.i('dve', 'scalar_tensor_tensor', acc, ub[:, k:k + T], dwv[:, j * CK + k:j * CK + k + 1], acc,
                        ALU.mult, ALU.add, reads=[ubk, 'dwv', ak], writes=[ak])
            for j in range(CC):
                s.i('pe', 'matmul', ps[4][:, 0:T], ones_f[:], yacc[:, j, 0:T], start=(j == 0), stop=(j == CC - 1),
                    reads=['ones_f', 'yacc%d' % j], writes=['ps4'])
            s.i('act', 'activation', out=wk[0][:, 0:T], in_=ps[4][:, 0:T], func=AF.Copy, scale=1.0 / CW, reads=['ps4'], writes=['wk0'])
            for j in range(CC):
                s.i('dve', 'tensor_tensor', yacc[:, j, 0:T], yacc[:, j, 0:T], wk[0][:, 0:T], ALU.subtract,
                    reads=['yacc%d' % j, 'wk0'], writes=['yacc%d' % j])
                s.i('act', 'activation', out=wk[1][:, 0:T], in_=yacc[:, j, 0:T], func=AF.Square, reads=['yacc%d' % j], writes=['wk1'])
                s.i('pe', 'matmul', ps[5][:, 0:T], ones_f[:], wk[1][:, 0:T], start=(j == 0), stop=(j == CC - 1),
                    reads=['ones_f', 'wk1'], writes=['ps5'])
            s.i('act', 'activation', out=rstd[:, 0:T], in_=ps[5][:, 0:T], func=AF.Sqrt, scale=1.0 / CW, bias=epsc[:, 0:1], reads=['ps5'], writes=['rstd'])
            s.i('dve', 'reciprocal', rstd[:, 0:T], rstd[:, 0:T], reads=['rstd'], writes=['rstd'])
            for j in range(CC):
                s.i('dve', 'tensor_tensor', wk[2][:, 0:T], yacc[:, j, 0:T], rstd[:, 0:T], ALU.mult, reads=['yacc%d' % j, 'rstd'], writes=['wk2'])
                s.i('act', 'activation', out=aT[:, j, 0:T], in_=wk[2][:, 0:T], func=AF.Silu,
                    scale=cvv[:, 1 * CC + j:1 * CC + j + 1], bias=cvv[:, 2 * CC + j:2 * CC + j + 1], reads=['wk2', 'cvv'], writes=['aT'])
            for oc in range(CC):
                sl, sk = load_slab(wb_pw, 0, CC, oc * 128, 128)
                p, pk = next_ps(0, 4)
                for kc in range(CC):
                    s.i('pe', 'matmul', p[:, 0:T], sl[:, kc, 0:128], aT[:, kc, 0:T], start=(kc == 0), stop=(kc == CC - 1),
                        reads=[sk, 'aT'], writes=[pk])
                b = cnt['wk'] % 2; cnt['wk'] += 1
                s.i('act', 'activation', out=wkb[2 + b][:, 0:T], in_=p[:, 0:T], func=AF.Identity, bias=cvv[:, 3 * CC + oc:3 * CC + oc + 1],
                    reads=[pk, 'cvv'], writes=['wkb%d' % (2 + b)])
                s.ld(YT[oc, :, t0:t0 + T], wkb[2 + b][:, 0:T], reads=['wkb%d' % (2 + b)])
        s.barrier()

        for dr in range(2):
            A, B_, LS, BR, BI = sF[0], sF[1], sF[2], sF[3], sF[4]
            s.ld(A[:], areF[l, dr], writes=['sF0']); s.ld(B_[:], aimF[l, dr], writes=['sF1']); s.ld(LS[:], lsF[l, dr], writes=['sF2'])
            s.ld(BR[:], breF[l, dr], writes=['sF3']); s.ld(BI[:], bimF[l, dr], writes=['sF4'])
            dt_, mag, x, lre, lim = sF[5], sF[6], sF[7], sF[8], sF[9]
            s.i('act', 'activation', out=dt_[:], in_=LS[:], func=AF.Exp, reads=['sF2'], writes=['sF5'])
            s.i('dve', 'tensor_tensor', mag[:], A[:], dt_[:], ALU.mult, reads=['sF0', 'sF5'], writes=['sF6'])
            s.i('act', 'activation', out=mag[:], in_=mag[:], func=AF.Exp, reads=['sF6'], writes=['sF6'])
            s.i('dve', 'tensor_tensor', x[:], B_[:], dt_[:], ALU.mult, reads=['sF1', 'sF5'], writes=['sF7'])
            s.i('dve', 'tensor_scalar', x[:], x[:], 1.0 / TWO_PI, None, ALU.mult, reads=['sF7'], writes=['sF7'])
            s.i('dve', 'tensor_copy', sFi[:], x[:], reads=['sF7'], writes=['sFi'])
            s.i('dve', 'tensor_copy', LS[:], sFi[:], reads=['sFi'], writes=['sF2'])
            s.i('dve', 'tensor_tensor', x[:], x[:], LS[:], ALU.subtract, reads=['sF7', 'sF2'], writes=['sF7'])
            s.i('act', 'activation', out=lim[:], in_=x[:], func=AF.Sin, scale=TWO_PI, reads=['sF7'], writes=['sF9'])
            s.i('dve', 'scalar_tensor_tensor', LS[:], x[:], -1.0, x[:], ALU.mult, ALU.max, reads=['sF7'], writes=['sF2'])
            s.i('act', 'activation', out=lre[:], in_=LS[:], func=AF.Sin, scale=-TWO_PI, bias=halfpi[:, 0:1], reads=['sF2'], writes=['sF8'])
            s.i('dve', 'tensor_tensor', lre[:], lre[:], mag[:], ALU.mult, reads=['sF8', 'sF6'], writes=['sF8'])
            s.i('dve', 'tensor_tensor', lim[:], lim[:], mag[:], ALU.mult, reads=['sF9', 'sF6'], writes=['sF9'])
            s.i('dve', 'tensor_scalar', lre[:], lre[:], -1.0, None, ALU.add, reads=['sF8'], writes=['sF8'])
            s.i('dve', 'tensor_tensor', dt_[:], A[:], A[:], ALU.mult, reads=['sF0'], writes=['sF5'])
            s.i('dve', 'tensor_tensor', mag[:], B_[:], B_[:], ALU.mult, reads=['sF1'], writes=['sF6'])
            s.i('dve', 'tensor_tensor', dt_[:], dt_[:], mag[:], ALU.add, reads=['sF5', 'sF6'], writes=['sF5'])
            s.i('dve', 'reciprocal', dt_[:], dt_[:], reads=['sF5'], writes=['sF5'])
            s.i('dve', 'tensor_tensor', mag[:], lre[:], A[:], ALU.mult, reads=['sF8', 'sF0'], writes=['sF6'])
            s.i('dve', 'tensor_tensor', LS[:], lim[:], B_[:], ALU.mult, reads=['sF9', 'sF1'], writes=['sF2'])
            s.i('dve', 'tensor_tensor', mag[:], mag[:], LS[:], ALU.add, reads=['sF6', 'sF2'], writes=['sF6'])
            s.i('dve', 'tensor_tensor', mag[:], mag[:], dt_[:], ALU.mult, reads=['sF6', 'sF5'], writes=['sF6'])
            s.i('dve', 'tensor_tensor', x[:], lim[:], A[:], ALU.mult, reads=['sF9', 'sF0'], writes=['sF7'])
            s.i('dve', 'tensor_tensor', LS[:], lre[:], B_[:], ALU.mult, reads=['sF8', 'sF1'], writes=['sF2'])
            s.i('dve', 'tensor_tensor', x[:], x[:], LS[:], ALU.subtract, reads=['sF7', 'sF2'], writes=['sF7'])
            s.i('dve', 'tensor_tensor', x[:], x[:], dt_[:], ALU.mult, reads=['sF7', 'sF5'], writes=['sF7'])
            fre, fim = mag, x
            s.i('dve', 'tensor_tensor', lre[:], fre[:], BR[:], ALU.mult, reads=['sF6', 'sF3'], writes=['sF8'])
            s.i('dve', 'tensor_tensor', LS[:], fim[:], BI[:], ALU.mult, reads=['sF7', 'sF4'], writes=['sF2'])
            s.i('dve', 'tensor_tensor', lre[:], lre[:], LS[:], ALU.subtract, reads=['sF8', 'sF2'], writes=['sF8'])
            s.i('dve', 'tensor_tensor', lim[:], fre[:], BI[:], ALU.mult, reads=['sF6', 'sF4'], writes=['sF9'])
            s.i('dve', 'tensor_tensor', LS[:], fim[:], BR[:], ALU.mult, reads=['sF7', 'sF3'], writes=['sF2'])
            s.i('dve', 'tensor_tensor', lim[:], lim[:], LS[:], ALU.add, reads=['sF9', 'sF2'], writes=['sF9'])
            bre3 = lre[:].rearrange("p (j q) -> p j q", q=64); bim3 = lim[:].rearrange("p (j q) -> p j q", q=64)
            s.i('dve', 'tensor_copy', Bd1[:, dr, :, 0:64], bre3, reads=['sF8'], writes=['Bd'])
            s.i('dve', 'tensor_copy', Bd1[:, dr, :, 64:128], bim3, reads=['sF9'], writes=['Bd'])
            s.i('dve', 'tensor_scalar', Bd2[:, dr, :, 0:64], bim3, -1.0, None, ALU.mult, reads=['sF9'], writes=['Bd'])
            s.i('dve', 'tensor_copy', Bd2[:, dr, :, 64:128], bre3, reads=['sF8'], writes=['Bd'])
            s.ld(CmD[:, dr].rearrange("p a b -> p (a b)"), cS[l, dr], writes=['CmD'])
            s.i('dve', 'tensor_scalar', CmD[64:128, dr], CmD[64:128, dr], -1.0, None, ALU.mult, reads=['CmD'], writes=['CmD'])
            a_, b_, l_ = sG[0], sG[1], sG[2]
            s.ld(a_[:], areS[l, dr], writes=['sG0']); s.ld(b_[:], aimS[l, dr], writes=['sG1']); s.ld(l_[:], lsS[l, dr], writes=['sG2'])
            s.i('act', 'activation', out=l_[:], in_=l_[:], func=AF.Exp, reads=['sG2'], writes=['sG2'])
            s.i('dve', 'tensor_tensor', a_[:], a_[:], l_[:], ALU.mult, reads=['sG0', 'sG2'], writes=['sG0'])
            s.i('act', 'activation', out=ssm_r[:, dr, :], in_=a_[:], func=AF.Exp, reads=['sG0'], writes=['ssm_r'])
            s.i('dve', 'tensor_tensor', b_[:], b_[:], l_[:], ALU.mult, reads=['sG1', 'sG2'], writes=['sG1'])
            s.i('dve', 'tensor_scalar', b_[:], b_[:], 1.0 / TWO_PI, None, ALU.mult, reads=['sG1'], writes=['sG1'])

            def gfrac(dst, src, kd, ks):
                s.i('dve', 'tensor_copy', sGi[:], src, reads=[ks], writes=['sGi'])
                s.i('dve', 'tensor_copy', sG[3][:], sGi[:], reads=['sGi'], writes=['sG3'])
                s.i('dve', 'tensor_tensor', dst, src, sG[3][:], ALU.subtract, reads=[ks, 'sG3'], writes=[kd])
            gfrac(ssm_phi[:, dr, :], b_[:], 'ssm_phi', 'sG1')
            s.i('dve', 'tensor_scalar', a_[:], ssm_phi[:, dr, :], 64.0, None, ALU.mult, reads=['ssm_phi'], writes=['sG0'])
            gfrac(ssm_phi64[:, dr, :], a_[:], 'ssm_phi64', 'sG0')
            for k, (t0, T) in enumerate(blocks):
                s.i('dve', 'tensor_scalar', a_[:], ssm_phi64[:, dr, :], float(t0 // 64), None, ALU.mult, reads=['ssm_phi64'], writes=['sG0'])
                gfrac(ssm_off[:, dr, :, k], a_[:], 'ssm_off', 'sG0')

        for j in range(SC):
            s.ld(zsn[:, 0:N], ZS[j], writes=['zsn'])
            s.i('pool', 'tensor_scalar', ysum[:, 0:N], zsn[:, 0:N], smv[:, j:j + 1], None, ALU.mult, reads=['zsn', 'smv'], writes=['ysum'])
            for dr in range(2):
                for gp in range(8):
                    s.i('pool', 'tensor_scalar', lhsB[:, (dr * 2 + 0) * 8 + gp, :], Bd1[:, dr, j, :], rowmask[:, gp:gp + 1], None, ALU.mult,
                        reads=['Bd', 'rowmask'], writes=['lhsB'])
                    s.i('pool', 'tensor_scalar', lhsB[:, (dr * 2 + 1) * 8 + gp, :], Bd2[:, dr, j, :], rowmask[:, gp:gp + 1], None, ALU.mult,
                        reads=['Bd', 'rowmask'], writes=['lhsB'])
                s.i('pool', 'memset', lhsC[:, dr * 8:(dr + 1) * 8, :], 0.0, writes=['lhsC'])
                for gp in range(8):
                    s.i('pool', 'tensor_copy', lhsC[:, dr * 8 + gp, gp * 16:(gp + 1) * 16], CmD[:, dr, j, gp * 16:(gp + 1) * 16],
                        reads=['CmD'], writes=['lhsC'])
            s.i('pool', 'memset', carry[:], 0.0, writes=['carry'])
            for k, (t0, T) in enumerate(blocks):
                for dr in range(2):
                    if dr == 0:
                        n0 = t0
                        src = zsn[:, n0:n0 + T]
                    else:
                        n0 = 0 if k == 0 else N - (t0 - NCX) - T
                        src = zsn[:, n0:n0 + T][:, ::-1]
                    zb = wkb[dr]; zbk = 'wkb%d' % dr
                    s.i('pool', 'tensor_copy', zb[:, 0:T], src, reads=['zsn'], writes=[zbk])
                    yp, ypk = ps[6 + dr], 'ps%d' % (6 + dr)
                    for gp in range(8):
                        g = j * 8 + gp
                        pa1, ka1 = next_ps(0, 6)
                        pa2, ka2 = next_ps(0, 6)
                        s.i('pe', 'matmul', pa1[:, 0:T], lhsB[:, (dr * 2 + 0) * 8 + gp, :], zb[:, 0:T], start=True, stop=True, reads=['lhsB', zbk], writes=[ka1])
                        s.i('pe', 'matmul', pa2[:, 0:T], lhsB[:, (dr * 2 + 1) * 8 + gp, :], zb[:, 0:T], start=True, stop=True, reads=['lhsB', zbk], writes=[ka2])
                        s.i('pool', 'tensor_scalar', wk[0][:, 0:T], iota[:, 0:T], ssm_phi[:, dr, g:g + 1], ssm_off[:, dr, g, k:k + 1],
                            ALU.mult, ALU.add, reads=['iota', 'ssm_phi', 'ssm_off'], writes=['wk0'])
                        frac(wk[1][:, 0:T], wk[0][:, 0:T], 'wk1', 'wk0', T, eng='pool')
                        sincos(wk[2][:, 0:T], wk[3][:, 0:T], wk[1][:, 0:T], 'wk2', 'wk3', 'wk1', T, wk[0][:, 0:T], 'wk0')
                        sn, cs = wk[2][:, 0:T], wk[3][:, 0:T]
                        s.i('dve', 'tensor_tensor', wk[4][:, 0:T], pa1[:, 0:T], cs, ALU.mult, reads=[ka1, 'wk3'], writes=['wk4'])
                        s.i('dve', 'tensor_tensor', wk[5][:, 0:T], pa2[:, 0:T], sn, ALU.mult, reads=[ka2, 'wk2'], writes=['wk5'])
                        s.i('dve', 'tensor_tensor', wk[4][:, 0:T], wk[4][:, 0:T], wk[5][:, 0:T], ALU.subtract, reads=['wk4', 'wk5'], writes=['wk4'])
                        s.i('dve', 'tensor_tensor', wk[5][:, 0:T], pa2[:, 0:T], cs, ALU.mult, reads=[ka2, 'wk3'], writes=['wk5'])
                        s.i('dve', 'tensor_tensor', wk[6][:, 0:T], pa1[:, 0:T], sn, ALU.mult, reads=[ka1, 'wk2'], writes=['wk6'])
                        s.i('dve', 'tensor_tensor', wk[5][:, 0:T], wk[5][:, 0:T], wk[6][:, 0:T], ALU.add, reads=['wk5', 'wk6'], writes=['wk5'])
                        s.i('pool', 'tensor_scalar', wk[6][:, 0:T], iota[:, 0:T], 0.0, ssm_r[:, dr, g:g + 1], ALU.mult, ALU.add,
                            reads=['iota', 'ssm_r'], writes=['wk6'])
                        s.i('dve', 'tensor_tensor_scan', wk[0][:, 0:T], wk[6][:, 0:T], wk[4][:, 0:T], carry[:, dr * 2 + 0, gp:gp + 1], ALU.mult, ALU.add,
                            reads=['wk6', 'wk4', 'carry'], writes=['wk0'])
                        s.i('dve', 'tensor_tensor_scan', wk[1][:, 0:T], wk[6][:, 0:T], wk[5][:, 0:T], carry[:, dr * 2 + 1, gp:gp + 1], ALU.mult, ALU.add,
                            reads=['wk6', 'wk5', 'carry'], writes=['wk1'])
                        s.i('dve', 'tensor_copy', carry[:, dr * 2 + 0, gp:gp + 1], wk[0][:, T - 1:T], reads=['wk0'], writes=['carry'])
                        s.i('dve', 'tensor_copy', carry[:, dr * 2 + 1, gp:gp + 1], wk[1][:, T - 1:T], reads=['wk1'], writes=['carry'])
                        s.i('dve', 'tensor_tensor', wk[4][:, 0:T], wk[0][:, 0:T], cs, ALU.mult, reads=['wk0', 'wk3'], writes=['wk4'])
                        s.i('dve', 'tensor_tensor', wk[5][:, 0:T], wk[1][:, 0:T], sn, ALU.mult, reads=['wk1', 'wk2'], writes=['wk5'])
                        sb = wkb[2 + gp % 2]; sbk = 'wkb%d' % (2 + gp % 2)
                        s.i('dve', 'tensor_tensor', sb[:, 0:T], wk[4][:, 0:T], wk[5][:, 0:T], ALU.add, reads=['wk4', 'wk5'], writes=[sbk])
                        s.i('pe', 'matmul', yp[:, 0:T], lhsC[:, dr * 8 + gp, :], sb[:, 0:T], start=(gp == 0), stop=(gp == 7), reads=['lhsC', sbk], writes=[ypk])
                    dst = ysum[:, n0:n0 + T] if dr == 0 else ysum[:, n0:n0 + T][:, ::-1]
                    s.i('dve', 'tensor_tensor', dst, dst, yp[:, 0:T], ALU.add, reads=['ysum', ypk], writes=['ysum'])
            s.ld(YS[j], ysum[:, 0:N], reads=['ysum'])
        s.barrier()
        for bi, (t0, T) in enumerate(blocks):
            for j in range(SC):
                s.ld(wk[0][:, 0:T], YS[j, :, t0:t0 + T], writes=['wk0'])
                s.i('act', 'activation', out=yacc[:, j, 0:T], in_=wk[0][:, 0:T], func=AF.Gelu, reads=['wk0'], writes=['yacc%d' % j])
                s.i('dve', 'tensor_copy', aT[:, j, 0:T], yacc[:, j, 0:T], reads=['yacc%d' % j], writes=['aT'])
            for oc in range(SC):
                sl, sk = load_slab(wb_glu, 0, SC, oc * 128, 128)
                p, pk = next_ps(0, 4)
                for kc in range(SC):
                    s.i('pe', 'matmul', p[:, 0:T], sl[:, kc, 0:128], aT[:, kc, 0:T], start=(kc == 0), stop=(kc == SC - 1),
                        reads=[sk, 'aT'], writes=[pk])
                s.i('act', 'activation', out=wk[1][:, 0:T], in_=p[:, 0:T], func=AF.Sigmoid, bias=smv[:, SC + oc:SC + oc + 1],
                    reads=[pk, 'smv'], writes=['wk1'])
                b = cnt['wk'] % 2; cnt['wk'] += 1
                s.i('dve', 'tensor_tensor', wkb[2 + b][:, 0:T], wk[1][:, 0:T], yacc[:, oc, 0:T], ALU.mult, reads=['wk1', 'yacc%d' % oc], writes=['wkb%d' % (2 + b)])
                s.ld(YT[CC + oc, :, t0:t0 + T], wkb[2 + b][:, 0:T], reads=['wkb%d' % (2 + b)])
        s.barrier()

        for hh in range(H):
            s.ld(kth[:, 0:N], KT[hh], writes=['kth'])
            for k0 in range(0, NKT, 16):
                k1 = min(NKT, k0 + 16)
                s.ld(vth[:, k0 * 128:k1 * 128].rearrange("p (t e) -> p t e", e=128),
                     V[k0 * 128:k1 * 128, hh * 128:(hh + 1) * 128].rearrange("(t p) e -> p t e", p=128), writes=['vth'])
            for bi, (t0, T) in enumerate(blocks):
                qb = wkb[bi % 2]; qbk = 'wkb%d' % (bi % 2)
                s.ld(qb[:, 0:T], QT[hh, :, t0:t0 + T], writes=[qbk])
                nkt = NCX // 128 if bi == 0 else NKT
                for n in range(2):
                    op_, opk = ps[4 + n], 'ps%d' % (4 + n)
                    dp_, dpk = ps[6 + n], 'ps%d' % (6 + n)
                    for kt in range(nkt):
                        sp_, spk = next_ps(0, 4)
                        s.i('pe', 'matmul', sp_[:, 0:T], kth[n * 64:(n + 1) * 64, kt * 128:(kt + 1) * 128], qb[n * 64:(n + 1) * 64, 0:T],
                            start=True, stop=True, reads=['kth', qbk], writes=[spk])
                        b = cnt['wk'] % 2; cnt['wk'] += 1
                        pb = wkb[2 + b]; pbk = 'wkb%d' % (2 + b)
                        s.i('act', 'activation', out=pb[:, 0:T], in_=sp_[:, 0:T], func=AF.Exp, scale=0.125, reads=[spk], writes=[pbk])
                        s.i('pe', 'matmul', op_[:, 0:T], vth[:, kt * 128:(kt + 1) * 128], pb[:, 0:T], start=(kt == 0), stop=(kt == nkt - 1),
                            reads=['vth', pbk], writes=[opk])
                        s.i('pe', 'matmul', dp_[:, 0:T], ones_bf[:], pb[:, 0:T], start=(kt == 0), stop=(kt == nkt - 1),
                            reads=['ones', pbk], writes=[dpk])
                s.i('dve', 'reciprocal', wk[0][:, 0:T], ps[6][:, 0:T], reads=['ps6'], writes=['wk0'])
                s.i('dve', 'reciprocal', wk[1][:, 0:T], ps[7][:, 0:T], reads=['ps7'], writes=['wk1'])
                s.i('dve', 'tensor_tensor', wk[0][:, 0:T], wk[0][:, 0:T], ps[4][:, 0:T], ALU.mult, reads=['wk0', 'ps4'], writes=['wk0'])
                s.i('dve', 'tensor_tensor', wk[1][:, 0:T], wk[1][:, 0:T], ps[5][:, 0:T], ALU.mult, reads=['wk1', 'ps5'], writes=['wk1'])
                s.i('dve', 'scalar_tensor_tensor', wk[2][:, 0:T], wk[1][:, 0:T], nlam[:, 0:1], wk[0][:, 0:T], ALU.mult, ALU.add,
                    reads=['wk0', 'wk1', 'nlam'], writes=['wk2'])
                s.i('act', 'activation', out=wk[3][:, 0:T], in_=wk[2][:, 0:T], func=AF.Square, reads=['wk2'], writes=['wk3'])
                sp_, spk = next_ps(0, 4)
                s.i('pe', 'matmul', sp_[:, 0:T], ones_f[:], wk[3][:, 0:T], start=True, stop=True, reads=['ones_f', 'wk3'], writes=[spk])
                s.i('act', 'activation', out=wk[4][:, 0:T], in_=sp_[:, 0:T], func=AF.Sqrt, scale=1.0 / 128, bias=epsc[:, 0:1], reads=[spk], writes=['wk4'])
                s.i('dve', 'reciprocal', wk[4][:, 0:T], wk[4][:, 0:T], reads=['wk4'], writes=['wk4'])
                s.i('dve', 'tensor_tensor', wk[5][:, 0:T], wk[2][:, 0:T], wk[4][:, 0:T], ALU.mult, reads=['wk2', 'wk4'], writes=['wk5'])
                ob = wkb[4 + bi % 2]; obk = 'wkb%d' % (4 + bi % 2)
                s.i('act', 'activation', out=ob[:, 0:T], in_=wk[5][:, 0:T], func=AF.Identity, scale=subg[:, 0:1],
                    reads=['wk5', 'subg'], writes=[obk])
                s.ld(YT[CC + SC + hh, :, t0:t0 + T], ob[:, 0:T], reads=[obk])
        s.barrier()

        for bi, (t0, T) in enumerate(blocks):
            w = 1 if bi == 0 else 0
            for kc in range(MC):
                s.ld(aT[:, kc, 0:T], YT[kc, :, t0:t0 + T], writes=['aT'])
            for oc0 in range(0, DC, 2):
                sl, sk = load_slab(wb_out, 0, MC, oc0 * 128, 256)
                for o in range(2):
                    oc = oc0 + o
                    p, pk = next_ps(0, 6)
                    for kc in range(MC):
                        s.i('pe', 'matmul', p[:, 0:T], sl[:, kc, o * 128:(o + 1) * 128], aT[:, kc, 0:T], start=(kc == 0), stop=(kc == MC - 1),
                            reads=[sk, 'aT'], writes=[pk])
                    s.i('act', 'activation', out=yacc[:, oc, 0:T], in_=p[:, 0:T], func=AF.Copy, reads=[pk], writes=['yacc%d' % oc])
            stats_rstd(lambda c: (yacc[:, c, 0:T], 'yacc%d' % c), DC, T, float(D))
            for dc in range(DC):
                b = dc % 2
                s.ld(wk[b][:, 0:T], XT[dc, :, t0:t0 + T], writes=['wk%d' % b])
                s.i('dve', 'tensor_tensor', yacc[:, dc, 0:T], yacc[:, dc, 0:T], rstd[:, 0:T], ALU.mult, reads=['yacc%d' % dc, 'rstd'], writes=['yacc%d' % dc])
                s.i('dve', 'scalar_tensor_tensor', yacc[:, dc, 0:T], yacc[:, dc, 0:T], gg1[:, dc, w:w + 1], wk[b][:, 0:T], ALU.mult, ALU.add,
                    reads=['yacc%d' % dc, 'gsx', 'wk%d' % b], writes=['yacc%d' % dc])
                s.ld(XT[dc, :, t0:t0 + T], yacc[:, dc, 0:T], reads=['yacc%d' % dc], writes=['XT%d' % dc])
            stats_rstd(lambda c: (yacc[:, c, 0:T], 'yacc%d' % c), DC, T, float(D))
            for dc in range(DC):
                b = dc % 2
                s.i('dve', 'tensor_tensor', wk[b][:, 0:T], yacc[:, dc, 0:T], rstd[:, 0:T], ALU.mult, reads=['yacc%d' % dc, 'rstd'], writes=['wk%d' % b])
                s.i('act', 'activation', out=hT[:, dc, 0:T], in_=wk[b][:, 0:T], func=AF.Identity,
                    scale=gs2[:, dc, w:w + 1], bias=mT[:, 3 * DC + dc, w:w + 1], reads=['wk%d' % b, 'gsx', 'mT'], writes=['hT'])
            for q in range(NQ):
                for f0 in range(0, QFC, 2):
                    sl, sk = load_slab(wb_1, 0, DC, q * QF + f0 * 128, 256)
                    for o in range(2):
                        p, pk = next_ps(0, 6)
                        for dc in range(DC):
                            s.i('pe', 'matmul', p[:, 0:T], sl[:, dc, o * 128:(o + 1) * 128], hT[:, dc, 0:T], start=(dc == 0), stop=(dc == DC - 1),
                                reads=[sk, 'hT'], writes=[pk])
                        b = (f0 + o) % 2
                        s.i('act', 'activation', out=wk[2 + b][:, 0:T], in_=p[:, 0:T], func=AF.Relu, reads=[pk], writes=['wk%d' % (2 + b)])
                        s.i('pool', 'tensor_tensor', aT[:, f0 + o, 0:T], wk[2 + b][:, 0:T], wk[2 + b][:, 0:T], ALU.mult,
                            reads=['wk%d' % (2 + b)], writes=['aT'])
                for oc0 in range(0, DC, 2):
                    sl, sk = load_slab(wb_2, q * QF, QFC, oc0 * 128, 256)
                    for o in range(2):
                        oc = oc0 + o
                        p, pk = next_ps(0, 6)
                        for fc in range(QFC):
                            s.i('pe', 'matmul', p[:, 0:T], sl[:, fc, o * 128:(o + 1) * 128], aT[:, fc, 0:T], start=(fc == 0), stop=(fc == QFC - 1),
                                reads=[sk, 'aT'], writes=[pk])
                        if q == 0:
                            s.i('act', 'activation', out=yacc[:, oc, 0:T], in_=p[:, 0:T], func=AF.Copy, reads=[pk], writes=['yacc%d' % oc])
                        else:
                            s.i('dve', 'tensor_tensor', yacc[:, oc, 0:T], yacc[:, oc, 0:T], p[:, 0:T], ALU.add, reads=['yacc%d' % oc, pk], writes=['yacc%d' % oc])
            stats_rstd(lambda c: (yacc[:, c, 0:T], 'yacc%d' % c), DC, T, float(D))
            last = False
            for dc in range(DC):
                b = dc % 2
                s.ld(wk[b][:, 0:T], XT[dc, :, t0:t0 + T], reads=['XT%d' % dc], writes=['wk%d' % b])
                s.i('dve', 'tensor_tensor', yacc[:, dc, 0:T], yacc[:, dc, 0:T], rstd[:, 0:T], ALU.mult, reads=['yacc%d' % dc, 'rstd'], writes=['yacc%d' % dc])
                s.i('dve', 'scalar_tensor_tensor', yacc[:, dc, 0:T], yacc[:, dc, 0:T], gg2[:, dc, w:w + 1], wk[b][:, 0:T], ALU.mult, ALU.add,
                    reads=['yacc%d' % dc, 'gsx', 'wk%d' % b], writes=['yacc%d' % dc])
                if last:
                    if bi > 0:
                        s.ld(outT[dc, :, t0 - NCX:t0 - NCX + T], yacc[:, dc, 0:T], reads=['yacc%d' % dc])
                else:
                    s.ld(XT[dc, :, t0:t0 + T], yacc[:, dc, 0:T], reads=['yacc%d' % dc])
        s.barrier()

    s.emit()
    es.close()
    return nc


def prep_inputs(inp, cfg):
    D = cfg['D']; NL = cfg['NL']; NCX = cfg['NCX']; L = cfg['L']; GW = cfg['GW']
    DC = D // 128; N = NL + NCX; CW = D // 4; CC = CW // 128; SW = D // 4; SC = SW // 128; G = SW // 16
    f = lambda a: np.ascontiguousarray(np.asarray(a, dtype=np.float32))
    m = {}
    xall = np.concatenate([np.asarray(inp['ctx'])[0], np.asarray(inp['x'])[0]], axis=0)
    m['xT'] = f(xall.T.reshape(DC, 128, N))
    cc = np.stack([np.asarray(inp['c'])[0], np.asarray(inp['c_ctx'])], axis=-1)
    m['cT'] = f(cc.reshape(DC, 128, 2).transpose(1, 0, 2))
    for k in ('mod_down', 'mod_up', 'w_in', 'conv_pw', 'ssm_glu_w', 'w_out', 'mlp_w1', 'mlp_w2'):
        m[k] = f(inp[k])
    m['mod_bT'] = f(np.asarray(inp['mod_b']).reshape(L, 6 * DC, 128).transpose(0, 2, 1))
    m['norm_gT'] = f(np.asarray(inp['norm_g']).reshape(L, 4 * DC, 128).transpose(0, 2, 1))
    m['conv_dwT'] = f(np.asarray(inp['conv_dw']).reshape(L, CK, CC, 128).transpose(0, 3, 2, 1).reshape(L, 128, CC * CK))
    cv = np.stack([np.asarray(inp[k]) for k in ('conv_dw_b', 'conv_ln_g', 'conv_ln_b', 'conv_pw_b')], axis=1)
    m['convv'] = f(cv.reshape(L, 4, CC, 128).transpose(0, 3, 1, 2).reshape(L, 128, 4 * CC))

    def Flay(a):
        a = np.asarray(a).reshape(L, 2, SC, 8, 1, 64)
        a = np.broadcast_to(a, (L, 2, SC, 8, 16, 64))
        return f(a.transpose(0, 1, 3, 4, 2, 5).reshape(L, 2, 128, SC * 64))
    m['areF'] = Flay(inp['ssm_a_re']); m['aimF'] = Flay(inp['ssm_a_im'])
    m['lsF'] = Flay(np.broadcast_to(np.asarray(inp['ssm_log_step'])[..., None], (L, 2, G, 64)))

    def Blay(b):
        b = np.asarray(b).reshape(L, 2, SC, 8, 64, 16)
        return f(b.transpose(0, 1, 3, 5, 2, 4).reshape(L, 2, 128, SC * 64))
    m['breF'] = Blay(inp['ssm_b_re']); m['bimF'] = Blay(inp['ssm_b_im'])
    cre = np.asarray(inp['ssm_c_re']).reshape(L, 2, SC, 8, 16, 64)
    cim = np.asarray(inp['ssm_c_im']).reshape(L, 2, SC, 8, 16, 64)
    cre = cre.transpose(0, 1, 5, 2, 3, 4).reshape(L, 2, 64, SC * 128)
    cim = cim.transpose(0, 1, 5, 2, 3, 4).reshape(L, 2, 64, SC * 128)
    m['cS'] = f(np.concatenate([cre, cim], axis=2))

    def Slay(a):
        a = np.asarray(a).transpose(0, 1, 3, 2)
        return f(np.concatenate([a, a], axis=2))
    m['areS'] = Slay(inp['ssm_a_re']); m['aimS'] = Slay(inp['ssm_a_im'])
    m['lsS'] = f(np.broadcast_to(np.asarray(inp['ssm_log_step'])[:, :, None, :], (L, 2, 128, G)))
    sv = np.stack([np.asarray(inp['ssm_d']), np.asarray(inp['ssm_glu_b'])], axis=1)
    m['ssmv'] = f(sv.reshape(L, 2, SC, 128).transpose(0, 3, 1, 2).reshape(L, 128, 2 * SC))
    m['lamv'] = f(np.stack([np.asarray(inp[k]) for k in ('lam_q1', 'lam_k1', 'lam_q2', 'lam_k2')], axis=1))
    m['sublnT'] = f(np.asarray(inp['attn_subln_g']).reshape(L, 128, 1))
    tok = np.arange(NL)
    row = (tok // GW).astype(np.float32); col = (tok % GW).astype(np.float32)
    pos = np.zeros((128, N), np.float32)
    for p in range(128):
        pos[p, NCX:] = row if (p % 64) < 32 else col
    m['posT'] = pos
    m['jfreq'] = (np.arange(128) % 16).astype(np.float32).reshape(128, 1)
    rm = np.zeros((128, 8), np.float32)
    for r in range(128):
        rm[r, r // 16] = 1.0
    m['rowmask'] = rm
    m['iota512'] = f(np.broadcast_to(np.arange(512, dtype=np.float32)[None], (128, 512)))
    R = np.zeros((128, 128), np.float32)
    for p in range(128):
        if (p % 32) < 16:
            R[p + 16, p] = -1.0
        else:
            R[p - 16, p] = 1.0
    m['ropeR'] = R
    return m


LAYER_KEYS = ('mod_down', 'mod_up', 'mod_bT', 'norm_gT', 'w_in', 'conv_pw', 'ssm_glu_w', 'w_out', 'mlp_w1', 'mlp_w2',
              'conv_dwT', 'convv', 'areF', 'aimF', 'lsF', 'breF', 'bimF', 'cS', 'areS', 'aimS', 'lsS', 'ssmv', 'lamv', 'sublnT')


def run(inp, cfg, debug=False):
    L = cfg['L']
    cfg1 = dict(cfg); cfg1['L'] = 1
    nc = build(cfg1, debug)
    m = prep_inputs(inp, cfg)
    xT = m['xT']
    r = None
    for l in range(L):
        ml = {k: v for k, v in m.items() if k not in LAYER_KEYS}
        for k in LAYER_KEYS:
            ml[k] = np.ascontiguousarray(m[k][l:l + 1])
        ml['xT'] = xT
        li = 0.8 - 0.6 * math.exp(-0.3 * l)
        ml['laminit'] = np.tile(np.array([[li, 1.0 - li]], np.float32), (128, 1))
        res = run_bass_kernel_spmd(nc, [ml], core_ids=[0])
        r = res.results[0]
        xT = np.ascontiguousarray(np.asarray(r['XT'], dtype=np.float32))
    D = cfg['D']; NL = cfg['NL']; NCX = cfg['NCX']
    out = xT.reshape(D, NL + NCX)[:, NCX:].T[None]
    return np.ascontiguousarray(out.astype(np.float32)), r


def kernel(**inputs):
    out, _ = run(inputs, FULL)
    return out
```

```python
import math
from contextlib import ExitStack

import numpy as np
import concourse.bass as bass
import concourse.mybir as mybir
from concourse.bass_utils import run_bass_kernel_spmd

F32 = mybir.dt.float32
BF16 = mybir.dt.bfloat16
I32 = mybir.dt.int32
AF = mybir.ActivationFunctionType
ALU = mybir.AluOpType

COMPUTE = ('pe', 'act', 'dve', 'pool')
NDMASEM = 8
TWO_PI = 2.0 * math.pi


class Ev:
    __slots__ = ('sem', 'val')

    def __init__(self, sem, val):
        self.sem = sem
        self.val = val


class Sched:
    def __init__(self, nc, dma_queues=('sp', 'pool', 'act')):
        self.nc = nc
        self.streams = {e: [] for e in ('pe', 'act', 'dve', 'pool', 'sp')}
        self.cnt = {e: 0 for e in COMPUTE}
        self.semnames = ['c_' + e for e in COMPUTE]
        self.dq = {}
        for q in dma_queues:
            self.dq[q] = dict(i=0, cnt=[0] * NDMASEM)
            for k in range(NDMASEM):
                self.semnames.append('d_%s_%d' % (q, k))
        self.known = {e: {} for e in self.streams}
        self.track = {}
        self.rr = 0

    def _deps(self, reads, writes):
        evs = []
        for k in reads:
            t = self.track.get(k)
            if t and t['w'] is not None:
                evs.append(t['w'])
        for k in writes:
            t = self.track.get(k)
            if t:
                if t['w'] is not None:
                    evs.append(t['w'])
                evs.extend(t['r'])
        return evs

    def _commit(self, ev, reads, writes):
        for k in reads:
            t = self.track.setdefault(k, {'w': None, 'r': []})
            t['r'].append(ev)
            if len(t['r']) > 64:
                best = {}
                for e in t['r']:
                    if e.val > best.get(e.sem, Ev(e.sem, 0)).val:
                        best[e.sem] = e
                t['r'] = list(best.values())
        for k in writes:
            self.track[k] = {'w': ev, 'r': []}

    def _emit_waits(self, eng, evs):
        need = {}
        for ev in evs:
            if ev.val > need.get(ev.sem, 0):
                need[ev.sem] = ev.val
        kn = self.known[eng]
        for sname, v in need.items():
            if kn.get(sname, 0) >= v:
                continue
            kn[sname] = v
            self.streams[eng].append(('wait', sname, v))

    def op(self, eng, fn, reads=(), writes=()):
        reads = [k for k in reads if k is not None]
        writes = [k for k in writes if k is not None]
        evs = self._deps(reads, writes)
        if eng == 'pe':
            evs = [e for e in evs if e.sem != 'c_pe']
        self._emit_waits(eng, evs)
        self.cnt[eng] += 1
        ev = Ev('c_' + eng, self.cnt[eng])
        self.streams[eng].append(('op', fn, ev.sem))
        self._commit(ev, reads, writes)
        return ev

    def i(self, eng, meth, *args, reads=(), writes=(), **kw):
        return self.op(eng, lambda e: getattr(e, meth)(*args, **kw), reads, writes)

    def dma(self, q, out, in_, reads=(), writes=(), **kw):
        reads = [k for k in reads if k is not None]
        writes = [k for k in writes if k is not None]
        st = self.dq[q]
        slot = st['i'] % NDMASEM
        st['i'] += 1
        sem = 'd_%s_%d' % (q, slot)
        evs = self._deps(reads, writes)
        if st['cnt'][slot] > 0:
            evs.append(Ev(sem, st['cnt'][slot]))
        self._emit_waits(q, evs)
        st['cnt'][slot] += 16
        ev = Ev(sem, st['cnt'][slot])
        self.streams[q].append(('dma', out, in_, sem, kw))
        self._commit(ev, reads, writes)
        return ev

    def ld(self, out, in_, reads=(), writes=(), **kw):
        self.rr += 1
        return self.dma('sp' if self.rr % 2 else 'pool', out, in_, reads, writes, **kw)

    def barrier(self):
        evs = [Ev('c_' + e, self.cnt[e]) for e in COMPUTE if self.cnt[e] > 0]
        for q, st in self.dq.items():
            for k in range(NDMASEM):
                if st['cnt'][k] > 0:
                    evs.append(Ev('d_%s_%d' % (q, k), st['cnt'][k]))
        for eng in self.streams:
            self._emit_waits(eng, evs)
        self.track = {}

    def emit(self):
        nc = self.nc
        with ExitStack() as es:
            sems = {n: es.enter_context(nc.semaphore(n)) for n in self.semnames}
            block = es.enter_context(nc.Block())
            streams = self.streams

            def run(engobj, name):
                pend = []
                for it in streams[name]:
                    if it[0] == 'wait':
                        pend.append(it)
                        continue
                    for w in pend[:-1]:
                        engobj.wait_ge(sems[w[1]], w[2])
                    if it[0] == 'op':
                        ins = it[1](engobj)
                        if pend:
                            ins._wait_ge(sems[pend[-1][1]], pend[-1][2])
                        ins.then_inc(sems[it[2]], 1)
                    else:
                        ins = engobj.dma_start(out=it[1], in_=it[2], **it[4])
                        if pend:
                            ins._wait_ge(sems[pend[-1][1]], pend[-1][2])
                        ins.then_inc(sems[it[3]], 16)
                    pend = []
                for w in pend:
                    engobj.wait_ge(sems[w[1]], w[2])

            @block.tensor
            def _(e):
                run(e, 'pe')

            @block.scalar
            def _(e):
                run(e, 'act')

            @block.vector
            def _(e):
                run(e, 'dve')

            @block.gpsimd
            def _(e):
                run(e, 'pool')

            @block.sync
            def _(e):
                run(e, 'sp')


FULL = dict(D=4096, NL=8192, NCX=256, L=4, R=512, GW=64)
CK = 31
EPS = 1e-6


def build(cfg, debug=False):
    D = cfg['D']; NL = cfg['NL']; NCX = cfg['NCX']; L = cfg['L']; R = cfg['R']
    DC = D // 128; N = NL + NCX; TB = 512
    CW = D // 4; CC = CW // 128; SW = D // 4; SC = SW // 128; AW = D // 2; H = AW // 128
    FF = 4 * D; RC = R // 128; INW = 2 * CW + SW + 3 * AW; G = SW // 16
    MC = D // 128
    blocks = [(0, NCX)] + [(NCX + i * TB, TB) for i in range(NL // TB)]
    NBLK = len(blocks)
    NKT = N // 128
    QF = min(FF, cfg.get('QF', 4096)); NQ = FF // QF; QFC = QF // 128

    nc = bass.Bass("TRN2", target_bir_lowering=False)
    s = Sched(nc)
    es = ExitStack()

    def din(name, shape, dt=F32):
        return nc.dram_tensor(name, list(shape), dt, kind="ExternalInput").ap()

    def dscr(name, shape, dt=F32, dbg=False):
        return nc.dram_tensor(name, list(shape), dt, kind="ExternalOutput" if (dbg and debug) else "Internal").ap()

    xT_in = din("xT", [DC, 128, N])
    cT_in = din("cT", [128, DC, 2])
    mod_down = din("mod_down", [L, D, R]); mod_up = din("mod_up", [L, R, 6 * D])
    mod_bT = din("mod_bT", [L, 128, 6 * DC]); norm_gT = din("norm_gT", [L, 128, 4 * DC])
    w_in = din("w_in", [L, D, INW]); conv_pw = din("conv_pw", [L, CW, CW]); glu_w = din("ssm_glu_w", [L, SW, SW])
    w_out = din("w_out", [L, D, D]); w1 = din("mlp_w1", [L, D, FF]); w2 = din("mlp_w2", [L, FF, D])
    conv_dwT = din("conv_dwT", [L, 128, CC * CK]); convv = din("convv", [L, 128, 4 * CC])
    areF = din("areF", [L, 2, 128, SC * 64]); aimF = din("aimF", [L, 2, 128, SC * 64]); lsF = din("lsF", [L, 2, 128, SC * 64])
    breF = din("breF", [L, 2, 128, SC * 64]); bimF = din("bimF", [L, 2, 128, SC * 64])
    cS = din("cS", [L, 2, 128, SC * 128])
    areS = din("areS", [L, 2, 128, G]); aimS = din("aimS", [L, 2, 128, G]); lsS = din("lsS", [L, 2, 128, G])
    ssmv = din("ssmv", [L, 128, 2 * SC])
    lamv = din("lamv", [L, 4, 64]); sublnT = din("sublnT", [L, 128, 1])
    posT = din("posT", [128, N]); jfreq = din("jfreq", [128, 1])
    rowmask_in = din("rowmask", [128, 8]); iota_in = din("iota512", [128, 512]); ropeR_in = din("ropeR", [128, 128])
    laminit_in = din("laminit", [128, 2])

    XT = nc.dram_tensor("XT", [DC, 128, N], F32, kind="ExternalOutput").ap()
    UT = dscr("UT", [CC, 128, N], dbg=True)
    ZS = dscr("ZS", [SC, 128, N], dbg=True)
    YS = dscr("YS", [SC, 128, N], dbg=True)
    QT = dscr("QT", [H, 128, N], BF16); KT = dscr("KT", [H, 128, N], BF16)
    V = dscr("V", [N, AW], BF16)
    YT = dscr("YT", [MC, 128, N], BF16, dbg=True)
    wb_in = dscr("wb_in", [D, INW], BF16); wb_pw = dscr("wb_pw", [CW, CW], BF16); wb_glu = dscr("wb_glu", [SW, SW], BF16)
    wb_out = dscr("wb_out", [D, D], BF16); wb_1 = dscr("wb_1", [D, FF], BF16); wb_2 = dscr("wb_2", [FF, D], BF16)

    def SB(name, shape, dt=F32):
        return es.enter_context(nc.sbuf_tensor(name, list(shape), dt))

    ps = [es.enter_context(nc.psum_tensor("ps%d" % i, [128, 512], F32)) for i in range(8)]

    ones_bf = SB("ones_bf", [128, 128], BF16); ones_f = SB("ones_f", [128, 128])
    iota = SB("iota", [128, 512]); rowmask = SB("rowmask_sb", [128, 8]); ropeR = SB("ropeR_sb", [128, 128])
    sT = SB("sT", [128, DC, 2]); mT = SB("mT", [128, 6 * DC, 2]); mb = SB("mb", [128, 6 * DC])
    ng = SB("ng", [128, 4 * DC])
    gs1 = SB("gs1", [128, DC, 2]); gs2 = SB("gs2", [128, DC, 2]); gg1 = SB("gg1", [128, DC, 2]); gg2 = SB("gg2", [128, DC, 2])
    tT = SB("tT", [128, RC, 2])
    lamt = SB("lamt", [128, 4, 64]); lam = SB("lam", [128, 1]); nlam = SB("nlam", [128, 1]); subg = SB("subg", [128, 1])
    lsc = SB("lsc", [128, 4]); laminit = SB("laminit_sb", [128, 2])
    wk = [SB("wk%d" % i, [128, 512]) for i in range(8)]
    wki = SB("wki", [128, 512], I32)
    wkb = [SB("wkb%d" % i, [128, 512], BF16) for i in range(6)]
    rstd = SB("rstd", [128, 512])
    KCM = max(DC, QFC)
    YB = max(DC * 2048, 65536); HB = max((DC + 1) * 1024, ((N * 4 + 63) // 64) * 64); AB = max(KCM * 1024, ((N * 2 + 63) // 64) * 64); SLB = KCM * 512
    R_Y = 0; R_H = YB; R_A = R_H + HB; R_S = R_A + AB
    arena = SB("arena", [128, (YB + HB + AB + 2 * SLB) // 4])

    def view(off, shape, dt=F32):
        n = 1
        for d in shape:
            n *= d
        bpe = 4 if dt in (F32, I32) else 2
        assert off % 4 == 0
        v = arena[:, off // 4: off // 4 + (n * bpe + 3) // 4]
        if dt != F32:
            v = v.bitcast(dt)
        v = v[:, 0:n]
        if len(shape) == 2:
            return v.rearrange("p (a b) -> p a b", b=shape[1])
        if len(shape) == 3:
            return v.rearrange("p (a b c) -> p a b c", b=shape[1], c=shape[2])
        if len(shape) == 4:
            return v.rearrange("p (a b c d) -> p a b c d", b=shape[1], c=shape[2], d=shape[3])
        return v

    class Bump:
        def __init__(self, off, size):
            self.off = off; self.end = off + size

        def take(self, shape, dt=F32):
            n = 1
            for d in shape:
                n *= d
            nb = ((n * (4 if dt in (F32, I32) else 2) + 63) // 64) * 64
            v = view(self.off, shape, dt)
            self.off += nb
            assert self.off <= self.end, (self.off, self.end)
            return v

    yacc = view(R_Y, [DC, 512])
    hT = view(R_H, [DC + 1, 512], BF16)
    aT = view(R_A, [KCM, 512], BF16)
    slab = [view(R_S + i * SLB, [KCM, 256], BF16) for i in range(2)]
    cst = [view(R_Y + i * 8192, [2048]) for i in range(2)]
    cstb = [view(R_Y + 16384 + i * 4096, [2048], BF16) for i in range(2)]
    ubuf = [view(R_H + i * 2304, [544]) for i in range(2)]
    zsn = view(R_Y, [N])
    ysum = view(R_H, [N])
    bY = Bump(R_Y + ((N * 4 + 63) // 64) * 64, YB - ((N * 4 + 63) // 64) * 64)
    sF = [bY.take([SC * 64]) for i in range(10)]
    sFi = bY.take([SC * 64], I32)
    bA = Bump(R_A, AB + 2 * SLB)
    ssm_r = bA.take([2, G]); ssm_phi = bA.take([2, G]); ssm_phi64 = bA.take([2, G])
    ssm_off = bA.take([2, G, NBLK])
    Bd1 = bA.take([2, SC, 128]); Bd2 = bA.take([2, SC, 128]); CmD = bA.take([2, SC, 128])
    lhsB = bA.take([2 * 2 * 8, 128], BF16)
    lhsC = bA.take([2 * 8, 128], BF16)
    carry = bA.take([2 * 2, 8])
    sG = [bA.take([G]) for i in range(4)]
    sGi = bA.take([G], I32)
    kth = view(R_H, [N], BF16)
    vth = view(R_A, [NKT * 128], BF16)
    cosb = SB("cosb", [128, 512]); sinb = SB("sinb", [128, 512])
    COSD = dscr("COSD", [128, N]); SIND = dscr("SIND", [128, N])

    cnt = {'cast': 0, 'slab': 0, 'wk': 0}

    def frac(dst, src, key_dst, key_src, T, eng='dve'):
        s.i(eng, 'tensor_copy', wki[:, 0:T], src, reads=[key_src], writes=['wki'])
        s.i(eng, 'tensor_copy', wk[7][:, 0:T], wki[:, 0:T], reads=['wki'], writes=['wk7'])
        s.i(eng, 'tensor_tensor', dst, src, wk[7][:, 0:T], ALU.subtract, reads=[key_src, 'wk7'], writes=[key_dst])

    def sincos(sin_dst, cos_dst, r, ksin, kcos, kr, T, tmp, ktmp):
        s.i('act', 'activation', out=sin_dst, in_=r, func=AF.Sin, scale=TWO_PI, reads=[kr], writes=[ksin])
        s.i('dve', 'scalar_tensor_tensor', tmp, r, -1.0, r, ALU.mult, ALU.max, reads=[kr], writes=[ktmp])
        s.i('act', 'activation', out=cos_dst, in_=tmp, func=AF.Sin, scale=-TWO_PI, bias=halfpi[:, 0:1],
            reads=[ktmp], writes=[kcos])

    def cast_weight(src, dst, K, M):
        cw = [c for c in (2048, 1536, 1024, 512, 256, 128) if M % c == 0][0]
        for rc in range(K // 128):
            for c0 in range(0, M, cw):
                i = cnt['cast']; cnt['cast'] += 1
                b = i % 2
                s.ld(cst[b][:, 0:cw], src[rc * 128:(rc + 1) * 128, c0:c0 + cw], writes=['cst%d' % b])
                eng = ('act', 'pool', 'dve')[i % 3]
                if eng == 'act':
                    s.i('act', 'activation', out=cstb[b][:, 0:cw], in_=cst[b][:, 0:cw], func=AF.Copy,
                        reads=['cst%d' % b], writes=['cstb%d' % b])
                else:
                    s.i(eng, 'tensor_copy', cstb[b][:, 0:cw], cst[b][:, 0:cw], reads=['cst%d' % b], writes=['cstb%d' % b])
                s.ld(dst[rc * 128:(rc + 1) * 128, c0:c0 + cw], cstb[b][:, 0:cw], reads=['cstb%d' % b], writes=[])

    def load_slab(wb, r0, KCn, c0, cols):
        i = cnt['slab']; cnt['slab'] += 1
        b = i % 2
        for k0 in range(0, KCn, 8):
            k1 = min(KCn, k0 + 8)
            s.ld(slab[b][:, k0:k1, 0:cols],
                 wb[r0 + k0 * 128:r0 + k1 * 128, c0:c0 + cols].rearrange("(kc p) c -> p kc c", p=128),
                 writes=['slab%d' % b])
        return slab[b], 'slab%d' % b

    psrr = {'i': 0}

    def next_ps(lo=0, hi=8):
        i = lo + psrr['i'] % (hi - lo)
        psrr['i'] += 1
        return ps[i], 'ps%d' % i

    def stats_rstd(src_chunks, nch, T, denom):
        p, pk = ps[7], 'ps7'
        for c in range(nch):
            ap, k = src_chunks(c)
            b = c % 2
            s.i('act', 'activation', out=wkb[b][:, 0:T], in_=ap, func=AF.Square, reads=[k], writes=['wkb%d' % b])
            s.i('pe', 'matmul', p[:, 0:T], ones_bf[:], wkb[b][:, 0:T], start=(c == 0), stop=(c == nch - 1),
                reads=['wkb%d' % b, 'ones'], writes=[pk])
        s.i('act', 'activation', out=rstd[:, 0:T], in_=p[:, 0:T], func=AF.Sqrt, scale=1.0 / denom, bias=epsc[:, 0:1],
            reads=[pk], writes=['rstd'])
        s.i('dve', 'reciprocal', rstd[:, 0:T], rstd[:, 0:T], reads=['rstd'], writes=['rstd'])

    halfpi = SB("halfpi", [128, 1]); epsc = SB("epsc", [128, 1]); zero_c = SB("zero_c", [128, 1])
    s.i('pool', 'memset', halfpi[:], math.pi / 2, writes=['halfpi'])
    s.i('pool', 'memset', epsc[:], EPS, writes=['epsc'])
    s.i('pool', 'memset', zero_c[:], 0.0, writes=['zero_c'])
    s.i('pool', 'memset', ones_bf[:], 1.0, writes=['ones'])
    s.i('pool', 'memset', ones_f[:], 1.0, writes=['ones_f'])
    s.ld(iota[:], iota_in, writes=['iota'])
    s.ld(rowmask[:], rowmask_in, writes=['rowmask'])
    s.ld(ropeR[:], ropeR_in, writes=['ropeR'])
    s.ld(laminit[:], laminit_in, writes=['laminit'])
    for dc in range(DC):
        for t0 in range(0, N, 2048):
            t1 = min(N, t0 + 2048)
            b = cnt['cast'] % 2; cnt['cast'] += 1
            s.ld(cst[b][:, 0:t1 - t0], xT_in[dc, :, t0:t1], writes=['cst%d' % b])
            s.ld(XT[dc, :, t0:t1], cst[b][:, 0:t1 - t0], reads=['cst%d' % b])
    jf = SB("jf", [128, 1])
    s.ld(jf[:], jfreq, writes=['jf'])
    s.i('act', 'activation', out=jf[:], in_=jf[:], func=AF.Exp, scale=-math.log(10000.0) / 16.0, reads=['jf'], writes=['jf'])
    s.i('dve', 'tensor_scalar', jf[:], jf[:], 1.0 / TWO_PI, None, ALU.mult, reads=['jf'], writes=['jf'])
    for t0 in range(0, N, 512):
        T = min(512, N - t0)
        s.ld(wk[0][:, 0:T], posT[:, t0:t0 + T], writes=['wk0'])
        s.i('dve', 'tensor_scalar', wk[1][:, 0:T], wk[0][:, 0:T], jf[:, 0:1], None, ALU.mult, reads=['wk0', 'jf'], writes=['wk1'])
        frac(wk[2][:, 0:T], wk[1][:, 0:T], 'wk2', 'wk1', T)
        sincos(sinb[:, 0:T], cosb[:, 0:T], wk[2][:, 0:T], 'sinb', 'cosb', 'wk2', T, wk[3][:, 0:T], 'wk3')
        s.ld(SIND[:, t0:t0 + T], sinb[:, 0:T], reads=['sinb'])
        s.ld(COSD[:, t0:t0 + T], cosb[:, 0:T], reads=['cosb'])
    s.ld(sT[:], cT_in, writes=['sT'])
    s.i('act', 'activation', out=sT[:], in_=sT[:], func=AF.Silu, reads=['sT'], writes=['sT'])

    dwv = SB("dwv", [128, CC * CK]); cvv = SB("cvv", [128, 4 * CC]); smv = SB("smv", [128, 2 * SC])

    for l in range(L):
        cast_weight(w_in[l], wb_in, D, INW)
        cast_weight(conv_pw[l], wb_pw, CW, CW)
        cast_weight(glu_w[l], wb_glu, SW, SW)
        cast_weight(w_out[l], wb_out, D, D)
        cast_weight(w1[l], wb_1, D, FF)
        cast_weight(w2[l], wb_2, FF, D)
        s.ld(mb[:], mod_bT[l], writes=['mb'])
        s.ld(ng[:], norm_gT[l], writes=['ng'])
        for rc in range(RC):
            for d0 in range(0, DC, 8):
                dn = min(8, DC - d0)
                b = cnt['cast'] % 2; cnt['cast'] += 1
                s.ld(cst[b][:, 0:dn * 128].rearrange("p (a c) -> p a c", c=128),
                     mod_down[l, d0 * 128:(d0 + dn) * 128, rc * 128:(rc + 1) * 128].rearrange("(a p) c -> p a c", p=128),
                     writes=['cst%d' % b])
                for a in range(dn):
                    dc = d0 + a
                    s.i('pe', 'matmul', ps[0][:, 0:2], cst[b][:, a * 128:(a + 1) * 128], sT[:, dc, :],
                        start=(dc == 0), stop=(dc == DC - 1), reads=['cst%d' % b, 'sT'], writes=['ps0'])
            s.i('act', 'activation', out=tT[:, rc, :], in_=ps[0][:, 0:2], func=AF.Copy, reads=['ps0'], writes=['tT'])
        for m0 in range(0, 6 * DC, 4):
            b = cnt['cast'] % 2; cnt['cast'] += 1
            s.ld(cst[b][:, 0:RC * 512].rearrange("p (a c) -> p a c", c=512),
                 mod_up[l, :, m0 * 128:(m0 + 4) * 128].rearrange("(a p) c -> p a c", p=128), writes=['cst%d' % b])
            for mm in range(4):
                p, pk = next_ps(0, 4)
                for rc in range(RC):
                    s.i('pe', 'matmul', p[:, 0:2], cst[b][:, rc * 512 + mm * 128: rc * 512 + (mm + 1) * 128], tT[:, rc, :],
                        start=(rc == 0), stop=(rc == RC - 1), reads=['cst%d' % b, 'tT'], writes=[pk])
                s.i('dve', 'tensor_scalar', mT[:, m0 + mm, :], p[:, 0:2], mb[:, m0 + mm:m0 + mm + 1], None, ALU.add,
                    reads=[pk, 'mb'], writes=['mT'])
        for dc in range(DC):
            for (gsx, gix, scx) in ((gs1, 0, 1), (gs2, 2, 4)):
                s.i('dve', 'tensor_scalar', gsx[:, dc, :], mT[:, scx * DC + dc, :], 1.0, ng[:, gix * DC + dc:gix * DC + dc + 1],
                    ALU.add, ALU.mult, reads=['mT', 'ng'], writes=['gsx'])
            for (ggx, gix, gtx) in ((gg1, 1, 2), (gg2, 3, 5)):
                s.i('dve', 'tensor_scalar', ggx[:, dc, :], mT[:, gtx * DC + dc, :], ng[:, gix * DC + dc:gix * DC + dc + 1], None,
                    ALU.mult, reads=['mT', 'ng'], writes=['gsx'])
        s.ld(lamt[:].rearrange("p a b -> p (a b)"), lamv[l:l + 1].rearrange("o a b -> o (a b)").partition_broadcast(128), writes=['lamt'])
        s.ld(subg[:], sublnT[l], writes=['subg'])
        s.i('dve', 'tensor_tensor', wk[0][:, 0:64], lamt[:, 0, :], lamt[:, 1, :], ALU.mult, reads=['lamt'], writes=['wk0'])
        s.i('dve', 'tensor_tensor', wk[0][:, 64:128], lamt[:, 2, :], lamt[:, 3, :], ALU.mult, reads=['lamt'], writes=['wk0'])
        s.i('dve', 'reduce_sum', lsc[:, 0:1], wk[0][:, 0:64], mybir.AxisListType.X, reads=['wk0'], writes=['lsc'])
        s.i('dve', 'reduce_sum', lsc[:, 1:2], wk[0][:, 64:128], mybir.AxisListType.X, reads=['wk0'], writes=['lsc'])
        s.i('act', 'activation', out=lsc[:, 2:4], in_=lsc[:, 0:2], func=AF.Exp, reads=['lsc'], writes=['lsc'])
        s.i('dve', 'tensor_tensor', lam[:], lsc[:, 2:3], lsc[:, 3:4], ALU.subtract, reads=['lsc'], writes=['lam'])
        s.i('dve', 'tensor_scalar', lam[:], lam[:], laminit[:, 0:1], None, ALU.add, reads=['lam', 'laminit'], writes=['lam'])
        s.i('dve', 'tensor_scalar', nlam[:], lam[:], -1.0, None, ALU.mult, reads=['lam'], writes=['nlam'])
        s.i('dve', 'tensor_scalar', subg[:], subg[:], laminit[:, 1:2], None, ALU.mult, reads=['subg', 'laminit'], writes=['subg'])
        s.ld(dwv[:], conv_dwT[l], writes=['dwv']); s.ld(cvv[:], convv[l], writes=['cvv']); s.ld(smv[:], ssmv[l], writes=['smv'])

        s.barrier()

        for bi, (t0, T) in enumerate(blocks):
            w = 1 if bi == 0 else 0
            s.ld(cosb[:, 0:T], COSD[:, t0:t0 + T], writes=['cosb']); s.ld(sinb[:, 0:T], SIND[:, t0:t0 + T], writes=['sinb'])
            for dc in range(DC):
                s.ld(yacc[:, dc, 0:T], XT[dc, :, t0:t0 + T], writes=['yacc%d' % dc])
            stats_rstd(lambda c: (yacc[:, c, 0:T], 'yacc%d' % c), DC, T, float(D))
            for dc in range(DC):
                b = dc % 2
                s.i('dve', 'tensor_tensor', wk[b][:, 0:T], yacc[:, dc, 0:T], rstd[:, 0:T], ALU.mult,
                    reads=['yacc%d' % dc, 'rstd'], writes=['wk%d' % b])
                s.i('act', 'activation', out=hT[:, dc, 0:T], in_=wk[b][:, 0:T], func=AF.Identity,
                    scale=gs1[:, dc, w:w + 1], bias=mT[:, 0 * DC + dc, w:w + 1], reads=['wk%d' % b, 'gsx', 'mT'], writes=['hT'])
            NOC_FM = (2 * CW + SW + 2 * AW) // 128
            order = []
            for j in range(CC):
                order += [('a', j, j), ('g', j, CC + j)]
            for j in range(SC):
                order.append(('s', j, 2 * CC + j))
            for hh in range(H):
                order.append(('q', hh, 2 * CC + SC + hh))
            for hh in range(H):
                order.append(('k', hh, 2 * CC + SC + H + hh))
            for (kind, j, oc) in order:
                sl, sk = load_slab(wb_in, 0, DC, oc * 128, 128)
                p, pk = next_ps(0, 4)
                for dc in range(DC):
                    s.i('pe', 'matmul', p[:, 0:T], sl[:, dc, 0:128], hT[:, dc, 0:T], start=(dc == 0), stop=(dc == DC - 1),
                        reads=[sk, 'hT'], writes=[pk])
                if kind == 'a':
                    s.i('act', 'activation', out=wk[2][:, 0:T], in_=p[:, 0:T], func=AF.Copy, reads=[pk], writes=['wk2'])
                elif kind == 'g':
                    s.i('act', 'activation', out=wk[3][:, 0:T], in_=p[:, 0:T], func=AF.Sigmoid, reads=[pk], writes=['wk3'])
                    s.i('dve', 'tensor_tensor', wk[4][:, 0:T], wk[2][:, 0:T], wk[3][:, 0:T], ALU.mult, reads=['wk2', 'wk3'], writes=['wk4'])
                    s.ld(UT[j, :, t0:t0 + T], wk[4][:, 0:T], reads=['wk4'])
                elif kind == 's':
                    s.i('act', 'activation', out=wk[5][:, 0:T], in_=p[:, 0:T], func=AF.Copy, reads=[pk], writes=['wk5'])
                    s.ld(ZS[j, :, t0:t0 + T], wk[5][:, 0:T], reads=['wk5'])
                else:
                    s.i('act', 'activation', out=wk[2][:, 0:T], in_=p[:, 0:T], func=AF.Copy, reads=[pk], writes=['wk2'])
                    p2, pk2 = next_ps(4, 6)
                    s.i('pe', 'matmul', p2[:, 0:T], ropeR[:], wk[2][:, 0:T], start=True, stop=True, reads=['ropeR', 'wk2'], writes=[pk2])
                    s.i('dve', 'tensor_tensor', wk[3][:, 0:T], wk[2][:, 0:T], cosb[:, 0:T], ALU.mult, reads=['wk2', 'cosb'], writes=['wk3'])
                    s.i('dve', 'tensor_tensor', wk[4][:, 0:T], p2[:, 0:T], sinb[:, 0:T], ALU.mult, reads=[pk2, 'sinb'], writes=['wk4'])
                    b = cnt['wk'] % 2; cnt['wk'] += 1
                    s.i('dve', 'tensor_tensor', wkb[2 + b][:, 0:T], wk[3][:, 0:T], wk[4][:, 0:T], ALU.add, reads=['wk3', 'wk4'], writes=['wkb%d' % (2 + b)])
                    s.ld((QT if kind == 'q' else KT)[j, :, t0:t0 + T], wkb[2 + b][:, 0:T], reads=['wkb%d' % (2 + b)])
            vc0 = 2 * CW + SW + 2 * AW
            for c0 in range(0, AW, 256):
                sl, sk = load_slab(wb_in, 0, DC, vc0 + c0, 256)
                for tt in range(T // 128):
                    p, pk = next_ps(0, 4)
                    for dc in range(DC):
                        s.i('pe', 'matmul', p[:, 0:256], hT[:, dc, tt * 128:(tt + 1) * 128], sl[:, dc, 0:256],
                            start=(dc == 0), stop=(dc == DC - 1), reads=[sk, 'hT'], writes=[pk])
                    b = cnt['wk'] % 2; cnt['wk'] += 1
                    s.i('act', 'activation', out=wkb[2 + b][:, 0:256], in_=p[:, 0:256], func=AF.Copy, reads=[pk], writes=['wkb%d' % (2 + b)])
                    s.ld(V[t0 + tt * 128:t0 + (tt + 1) * 128, c0:c0 + 256], wkb[2 + b][:, 0:256], reads=['wkb%d' % (2 + b)])
        s.barrier()

        for bi, (t0, T) in enumerate(blocks):
            seg_lo, seg_hi = (0, NCX) if bi == 0 else (NCX, N)
            lo = max(seg_lo, t0 - 15); hi = min(seg_hi, t0 + T + 15)
            for j in range(CC):
                ub = ubuf[j % 2]; ubk = 'ub%d' % (j % 2)
                if lo > t0 - 15 or True:
                    s.i('pool', 'memset', ub[:, 0:T + 30], 0.0, writes=[ubk])
                s.ld(ub[:, lo - (t0 - 15): hi - (t0 - 15)], UT[j, :, lo:hi], writes=[ubk])
                acc = yacc[:, j, 0:T]; ak = 'yacc%d' % j
                s.i('dve', 'tensor_scalar', acc, ub[:, 0:T], dwv[:, j * CK:j * CK + 1], cvv[:, 0 * CC + j:0 * CC + j + 1],
                    ALU.mult, ALU.add, reads=[ubk, 'dwv', 'cvv'], writes=[ak])
                for k in range(1, CK):
                    s.i('dve', 'scalar_tensor_tensor', acc, ub[:, k:k + T], dwv[:, j * CK + k:j * CK + k + 1], acc,
                        ALU.mult, ALU.add, reads=[ubk, 'dwv', ak], writes=[ak])
            for j in range(CC):
                s.i('pe', 'matmul', ps[4][:, 0:T], ones_f[:], yacc[:, j, 0:T], start=(j == 0), stop=(j == CC - 1),
                    reads=['ones_f', 'yacc%d' % j], writes=['ps4'])
            s.i('act', 'activation', out=wk[0][:, 0:T], in_=ps[4][:, 0:T], func=AF.Copy, scale=1.0 / CW, reads=['ps4'], writes=['wk0'])
            for j in range(CC):
                s.i('dve', 'tensor_tensor', yacc[:, j, 0:T], yacc[:, j, 0:T], wk[0][:, 0:T], ALU.subtract,
                    reads=['yacc%d' % j, 'wk0'], writes=['yacc%d' % j])
                s.i('act', 'activation', out=wk[1][:, 0:T], in_=yacc[:, j, 0:T], func=AF.Square, reads=['yacc%d' % j], writes=['wk1'])
                s.i('pe', 'matmul', ps[5][:, 0:T], ones_f[:], wk[1][:, 0:T], start=(j == 0), stop=(j == CC - 1),
                    reads=['ones_f', 'wk1'], writes=['ps5'])
            s.i('act', 'activation', out=rstd[:, 0:T], in_=ps[5][:, 0:T], func=AF.Sqrt, scale=1.0 / CW, bias=epsc[:, 0:1], reads=['ps5'], writes=['rstd'])
            s.i('dve', 'reciprocal', rstd[:, 0:T], rstd[:, 0:T], reads=['rstd'], writes=['rstd'])
            for j in range(CC):
                s.i('dve', 'tensor_tensor', wk[2][:, 0:T], yacc[:, j, 0:T], rstd[:, 0:T], ALU.mult, reads=['yacc%d' % j, 'rstd'], writes=['wk2'])
                s.i('act', 'activation', out=aT[:, j, 0:T], in_=wk[2][:, 0:T], func=AF.Silu,
                    scale=cvv[:, 1 * CC + j:1 * CC + j + 1], bias=cvv[:, 2 * CC + j:2 * CC + j + 1], reads=['wk2', 'cvv'], writes=['aT'])
            for oc in range(CC):
                sl, sk = load_slab(wb_pw, 0, CC, oc * 128, 128)
                p, pk = next_ps(0, 4)
                for kc in range(CC):
                    s.i('pe', 'matmul', p[:, 0:T], sl[:, kc, 0:128], aT[:, kc, 0:T], start=(kc == 0), stop=(kc == CC - 1),
                        reads=[sk, 'aT'], writes=[pk])
                b = cnt['wk'] % 2; cnt['wk'] += 1
                s.i('act', 'activation', out=wkb[2 + b][:, 0:T], in_=p[:, 0:T], func=AF.Identity, bias=cvv[:, 3 * CC + oc:3 * CC + oc + 1],
                    reads=[pk, 'cvv'], writes=['wkb%d' % (2 + b)])
                s.ld(YT[oc, :, t0:t0 + T], wkb[2 + b][:, 0:T], reads=['wkb%d' % (2 + b)])
        s.barrier()

        for dr in range(2):
            A, B_, LS, BR, BI = sF[0], sF[1], sF[2], sF[3], sF[4]
            s.ld(A[:], areF[l, dr], writes=['sF0']); s.ld(B_[:], aimF[l, dr], writes=['sF1']); s.ld(LS[:], lsF[l, dr], writes=['sF2'])
            s.ld(BR[:], breF[l, dr], writes=['sF3']); s.ld(BI[:], bimF[l, dr], writes=['sF4'])
            dt_, mag, x, lre, lim = sF[5], sF[6], sF[7], sF[8], sF[9]
            s.i('act', 'activation', out=dt_[:], in_=LS[:], func=AF.Exp, reads=['sF2'], writes=['sF5'])
            s.i('dve', 'tensor_tensor', mag[:], A[:], dt_[:], ALU.mult, reads=['sF0', 'sF5'], writes=['sF6'])
            s.i('act', 'activation', out=mag[:], in_=mag[:], func=AF.Exp, reads=['sF6'], writes=['sF6'])
            s.i('dve', 'tensor_tensor', x[:], B_[:], dt_[:], ALU.mult, reads=['sF1', 'sF5'], writes=['sF7'])
            s.i('dve', 'tensor_scalar', x[:], x[:], 1.0 / TWO_PI, None, ALU.mult, reads=['sF7'], writes=['sF7'])
            s.i('dve', 'tensor_copy', sFi[:], x[:], reads=['sF7'], writes=['sFi'])
            s.i('dve', 'tensor_copy', LS[:], sFi[:], reads=['sFi'], writes=['sF2'])
            s.i('dve', 'tensor_tensor', x[:], x[:], LS[:], ALU.subtract, reads=['sF7', 'sF2'], writes=['sF7'])
            s.i('act', 'activation', out=lim[:], in_=x[:], func=AF.Sin, scale=TWO_PI, reads=['sF7'], writes=['sF9'])
            s.i('dve', 'scalar_tensor_tensor', LS[:], x[:], -1.0, x[:], ALU.mult, ALU.max, reads=['sF7'], writes=['sF2'])
            s.i('act', 'activation', out=lre[:], in_=LS[:], func=AF.Sin, scale=-TWO_PI, bias=halfpi[:, 0:1], reads=['sF2'], writes=['sF8'])
            s.i('dve', 'tensor_tensor', lre[:], lre[:], mag[:], ALU.mult, reads=['sF8', 'sF6'], writes=['sF8'])
            s.i('dve', 'tensor_tensor', lim[:], lim[:], mag[:], ALU.mult, reads=['sF9', 'sF6'], writes=['sF9'])
            s.i('dve', 'tensor_scalar', lre[:], lre[:], -1.0, None, ALU.add, reads=['sF8'], writes=['sF8'])
            s.i('dve', 'tensor_tensor', dt_[:], A[:], A[:], ALU.mult, reads=['sF0'], writes=['sF5'])
            s.i('dve', 'tensor_tensor', mag[:], B_[:], B_[:], ALU.mult, reads=['sF1'], writes=['sF6'])
            s.i('dve', 'tensor_tensor', dt_[:], dt_[:], mag[:], ALU.add, reads=['sF5', 'sF6'], writes=['sF5'])
            s.i('dve', 'reciprocal', dt_[:], dt_[:], reads=['sF5'], writes=['sF5'])
            s.i('dve', 'tensor_tensor', mag[:], lre[:], A[:], ALU.mult, reads=['sF8', 'sF0'], writes=['sF6'])
            s.i('dve', 'tensor_tensor', LS[:], lim[:], B_[:], ALU.mult, reads=['sF9', 'sF1'], writes=['sF2'])
            s.i('dve', 'tensor_tensor', mag[:], mag[:], LS[:], ALU.add, reads=['sF6', 'sF2'], writes=['sF6'])
            s.i('dve', 'tensor_tensor', mag[:], mag[:], dt_[:], ALU.mult, reads=['sF6', 'sF5'], writes=['sF6'])
            s.i('dve', 'tensor_tensor', x[:], lim[:], A[:], ALU.mult, reads=['sF9', 'sF0'], writes=['sF7'])
            s.i('dve', 'tensor_tensor', LS[:], lre[:], B_[:], ALU.mult, reads=['sF8', 'sF1'], writes=['sF2'])
            s.i('dve', 'tensor_tensor', x[:], x[:], LS[:], ALU.subtract, reads=['sF7', 'sF2'], writes=['sF7'])
            s.i('dve', 'tensor_tensor', x[:], x[:], dt_[:], ALU.mult, reads=['sF7', 'sF5'], writes=['sF7'])
            fre, fim = mag, x
            s.i('dve', 'tensor_tensor', lre[:], fre[:], BR[:], ALU.mult, reads=['sF6', 'sF3'], writes=['sF8'])
            s.i('dve', 'tensor_tensor', LS[:], fim[:], BI[:], ALU.mult, reads=['sF7', 'sF4'], writes=['sF2'])
            s.i('dve', 'tensor_tensor', lre[:], lre[:], LS[:], ALU.subtract, reads=['sF8', 'sF2'], writes=['sF8'])
            s.i('dve', 'tensor_tensor', lim[:], fre[:], BI[:], ALU.mult, reads=['sF6', 'sF4'], writes=['sF9'])
            s.i('dve', 'tensor_tensor', LS[:], fim[:], BR[:], ALU.mult, reads=['sF7', 'sF3'], writes=['sF2'])
            s.i('dve', 'tensor_tensor', lim[:], lim[:], LS[:], ALU.add, reads=['sF9', 'sF2'], writes=['sF9'])
            bre3 = lre[:].rearrange("p (j q) -> p j q", q=64); bim3 = lim[:].rearrange("p (j q) -> p j q", q=64)
            s.i('dve', 'tensor_copy', Bd1[:, dr, :, 0:64], bre3, reads=['sF8'], writes=['Bd'])
            s.i('dve', 'tensor_copy', Bd1[:, dr, :, 64:128], bim3, reads=['sF9'], writes=['Bd'])
            s.i('dve', 'tensor_scalar', Bd2[:, dr, :, 0:64], bim3, -1.0, None, ALU.mult, reads=['sF9'], writes=['Bd'])
            s.i('dve', 'tensor_copy', Bd2[:, dr, :, 64:128], bre3, reads=['sF8'], writes=['Bd'])
            s.ld(CmD[:, dr].rearrange("p a b -> p (a b)"), cS[l, dr], writes=['CmD'])
            s.i('dve', 'tensor_scalar', CmD[64:128, dr], CmD[64:128, dr], -1.0, None, ALU.mult, reads=['CmD'], writes=['CmD'])
            a_, b_, l_ = sG[0], sG[1], sG[2]
            s.ld(a_[:], areS[l, dr], writes=['sG0']); s.ld(b_[:], aimS[l, dr], writes=['sG1']); s.ld(l_[:], lsS[l, dr], writes=['sG2'])
            s.i('act', 'activation', out=l_[:], in_=l_[:], func=AF.Exp, reads=['sG2'], writes=['sG2'])
            s.i('dve', 'tensor_tensor', a_[:], a_[:], l_[:], ALU.mult, reads=['sG0', 'sG2'], writes=['sG0'])
            s.i('act', 'activation', out=ssm_r[:, dr, :], in_=a_[:], func=AF.Exp, reads=['sG0'], writes=['ssm_r'])
            s.i('dve', 'tensor_tensor', b_[:], b_[:], l_[:], ALU.mult, reads=['sG1', 'sG2'], writes=['sG1'])
            s.i('dve', 'tensor_scalar', b_[:], b_[:], 1.0 / TWO_PI, None, ALU.mult, reads=['sG1'], writes=['sG1'])

            def gfrac(dst, src, kd, ks):
                s.i('dve', 'tensor_copy', sGi[:], src, reads=[ks], writes=['sGi'])
                s.i('dve', 'tensor_copy', sG[3][:], sGi[:], reads=['sGi'], writes=['sG3'])
                s.i('dve', 'tensor_tensor', dst, src, sG[3][:], ALU.subtract, reads=[ks, 'sG3'], writes=[kd])
            gfrac(ssm_phi[:, dr, :], b_[:], 'ssm_phi', 'sG1')
            s.i('dve', 'tensor_scalar', a_[:], ssm_phi[:, dr, :], 64.0, None, ALU.mult, reads=['ssm_phi'], writes=['sG0'])
            gfrac(ssm_phi64[:, dr, :], a_[:], 'ssm_phi64', 'sG0')
            for k, (t0, T) in enumerate(blocks):
                s.i('dve', 'tensor_scalar', a_[:], ssm_phi64[:, dr, :], float(t0 // 64), None, ALU.mult, reads=['ssm_phi64'], writes=['sG0'])
                gfrac(ssm_off[:, dr, :, k], a_[:], 'ssm_off', 'sG0')

        for j in range(SC):
            s.ld(zsn[:, 0:N], ZS[j], writes=['zsn'])
            s.i('pool', 'tensor_scalar', ysum[:, 0:N], zsn[:, 0:N], smv[:, j:j + 1], None, ALU.mult, reads=['zsn', 'smv'], writes=['ysum'])
            for dr in range(2):
                for gp in range(8):
                    s.i('pool', 'tensor_scalar', lhsB[:, (dr * 2 + 0) * 8 + gp, :], Bd1[:, dr, j, :], rowmask[:, gp:gp + 1], None, ALU.mult,
                        reads=['Bd', 'rowmask'], writes=['lhsB'])
                    s.i('pool', 'tensor_scalar', lhsB[:, (dr * 2 + 1) * 8 + gp, :], Bd2[:, dr, j, :], rowmask[:, gp:gp + 1], None, ALU.mult,
                        reads=['Bd', 'rowmask'], writes=['lhsB'])
                s.i('pool', 'memset', lhsC[:, dr * 8:(dr + 1) * 8, :], 0.0, writes=['lhsC'])
                for gp in range(8):
                    s.i('pool', 'tensor_copy', lhsC[:, dr * 8 + gp, gp * 16:(gp + 1) * 16], CmD[:, dr, j, gp * 16:(gp + 1) * 16],
                        reads=['CmD'], writes=['lhsC'])
            s.i('pool', 'memset', carry[:], 0.0, writes=['carry'])
            for k, (t0, T) in enumerate(blocks):
                for dr in range(2):
                    if dr == 0:
                        n0 = t0
                        src = zsn[:, n0:n0 + T]
                    else:
                        n0 = 0 if k == 0 else N - (t0 - NCX) - T
                        src = zsn[:, n0:n0 + T][:, ::-1]
                    zb = wkb[dr]; zbk = 'wkb%d' % dr
                    s.i('pool', 'tensor_copy', zb[:, 0:T], src, reads=['zsn'], writes=[zbk])
                    yp, ypk = ps[6 + dr], 'ps%d' % (6 + dr)
                    for gp in range(8):
                        g = j * 8 + gp
                        pa1, ka1 = next_ps(0, 6)
                        pa2, ka2 = next_ps(0, 6)
                        s.i('pe', 'matmul', pa1[:, 0:T], lhsB[:, (dr * 2 + 0) * 8 + gp, :], zb[:, 0:T], start=True, stop=True, reads=['lhsB', zbk], writes=[ka1])
                        s.i('pe', 'matmul', pa2[:, 0:T], lhsB[:, (dr * 2 + 1) * 8 + gp, :], zb[:, 0:T], start=True, stop=True, reads=['lhsB', zbk], writes=[ka2])
                        s.i('pool', 'tensor_scalar', wk[0][:, 0:T], iota[:, 0:T], ssm_phi[:, dr, g:g + 1], ssm_off[:, dr, g, k:k + 1],
                            ALU.mult, ALU.add, reads=['iota', 'ssm_phi', 'ssm_off'], writes=['wk0'])
                        frac(wk[1][:, 0:T], wk[0][:, 0:T], 'wk1', 'wk0', T, eng='pool')
                        sincos(wk[2][:, 0:T], wk[3][:, 0:T], wk[1][:, 0:T], 'wk2', 'wk3', 'wk1', T, wk[0][:, 0:T], 'wk0')
                        sn, cs = wk[2][:, 0:T], wk[3][:, 0:T]
                        s.i('dve', 'tensor_tensor', wk[4][:, 0:T], pa1[:, 0:T], cs, ALU.mult, reads=[ka1, 'wk3'], writes=['wk4'])
                        s.i('dve', 'tensor_tensor', wk[5][:, 0:T], pa2[:, 0:T], sn, ALU.mult, reads=[ka2, 'wk2'], writes=['wk5'])
                        s.i('dve', 'tensor_tensor', wk[4][:, 0:T], wk[4][:, 0:T], wk[5][:, 0:T], ALU.subtract, reads=['wk4', 'wk5'], writes=['wk4'])
                        s.i('dve', 'tensor_tensor', wk[5][:, 0:T], pa2[:, 0:T], cs, ALU.mult, reads=[ka2, 'wk3'], writes=['wk5'])
                        s.i('dve', 'tensor_tensor', wk[6][:, 0:T], pa1[:, 0:T], sn, ALU.mult, reads=[ka1, 'wk2'], writes=['wk6'])
                        s.i('dve', 'tensor_tensor', wk[5][:, 0:T], wk[5][:, 0:T], wk[6][:, 0:T], ALU.add, reads=['wk5', 'wk6'], writes=['wk5'])
                        s.i('pool', 'tensor_scalar', wk[6][:, 0:T], iota[:, 0:T], 0.0, ssm_r[:, dr, g:g + 1], ALU.mult, ALU.add,
                            reads=['iota', 'ssm_r'], writes=['wk6'])
                        s.i('dve', 'tensor_tensor_scan', wk[0][:, 0:T], wk[6][:, 0:T], wk[4][:, 0:T], carry[:, dr * 2 + 0, gp:gp + 1], ALU.mult, ALU.add,
                            reads=['wk6', 'wk4', 'carry'], writes=['wk0'])
                        s.i('dve', 'tensor_tensor_scan', wk[1][:, 0:T], wk[6][:, 0:T], wk[5][:, 0:T], carry[:, dr * 2 + 1, gp:gp + 1], ALU.mult, ALU.add,
                            reads=['wk6', 'wk5', 'carry'], writes=['wk1'])
                        s.i('dve', 'tensor_copy', carry[:, dr * 2 + 0, gp:gp + 1], wk[0][:, T - 1:T], reads=['wk0'], writes=['carry'])
                        s.i('dve', 'tensor_copy', carry[:, dr * 2 + 1, gp:gp + 1], wk[1][:, T - 1:T], reads=['wk1'], writes=['carry'])
                        s.i('dve', 'tensor_tensor', wk[4][:, 0:T], wk[0][:, 0:T], cs, ALU.mult, reads=['wk0', 'wk3'], writes=['wk4'])
                        s.i('dve', 'tensor_tensor', wk[5][:, 0:T], wk[1][:, 0:T], sn, ALU.mult, reads=['wk1', 'wk2'], writes=['wk5'])
                        sb = wkb[2 + gp % 2]; sbk = 'wkb%d' % (2 + gp % 2)
                        s.i('dve', 'tensor_tensor', sb[:, 0:T], wk[4][:, 0:T], wk[5][:, 0:T], ALU.add, reads=['wk4', 'wk5'], writes=[sbk])
                        s.i('pe', 'matmul', yp[:, 0:T], lhsC[:, dr * 8 + gp, :], sb[:, 0:T], start=(gp == 0), stop=(gp == 7), reads=['lhsC', sbk], writes=[ypk])
                    dst = ysum[:, n0:n0 + T] if dr == 0 else ysum[:, n0:n0 + T][:, ::-1]
                    s.i('dve', 'tensor_tensor', dst, dst, yp[:, 0:T], ALU.add, reads=['ysum', ypk], writes=['ysum'])
            s.ld(YS[j], ysum[:, 0:N], reads=['ysum'])
        s.barrier()
        for bi, (t0, T) in enumerate(blocks):
            for j in range(SC):
                s.ld(wk[0][:, 0:T], YS[j, :, t0:t0 + T], writes=['wk0'])
                s.i('act', 'activation', out=yacc[:, j, 0:T], in_=wk[0][:, 0:T], func=AF.Gelu, reads=['wk0'], writes=['yacc%d' % j])
                s.i('dve', 'tensor_copy', aT[:, j, 0:T], yacc[:, j, 0:T], reads=['yacc%d' % j], writes=['aT'])
            for oc in range(SC):
                sl, sk = load_slab(wb_glu, 0, SC, oc * 128, 128)
                p, pk = next_ps(0, 4)
                for kc in range(SC):
                    s.i('pe', 'matmul', p[:, 0:T], sl[:, kc, 0:128], aT[:, kc, 0:T], start=(kc == 0), stop=(kc == SC - 1),
                        reads=[sk, 'aT'], writes=[pk])
                s.i('act', 'activation', out=wk[1][:, 0:T], in_=p[:, 0:T], func=AF.Sigmoid, bias=smv[:, SC + oc:SC + oc + 1],
                    reads=[pk, 'smv'], writes=['wk1'])
                b = cnt['wk'] % 2; cnt['wk'] += 1
                s.i('dve', 'tensor_tensor', wkb[2 + b][:, 0:T], wk[1][:, 0:T], yacc[:, oc, 0:T], ALU.mult, reads=['wk1', 'yacc%d' % oc], writes=['wkb%d' % (2 + b)])
                s.ld(YT[CC + oc, :, t0:t0 + T], wkb[2 + b][:, 0:T], reads=['wkb%d' % (2 + b)])
        s.barrier()

        for hh in range(H):
            s.ld(kth[:, 0:N], KT[hh], writes=['kth'])
            for k0 in range(0, NKT, 16):
                k1 = min(NKT, k0 + 16)
                s.ld(vth[:, k0 * 128:k1 * 128].rearrange("p (t e) -> p t e", e=128),
                     V[k0 * 128:k1 * 128, hh * 128:(hh + 1) * 128].rearrange("(t p) e -> p t e", p=128), writes=['vth'])
            for bi, (t0, T) in enumerate(blocks):
                qb = wkb[bi % 2]; qbk = 'wkb%d' % (bi % 2)
                s.ld(qb[:, 0:T], QT[hh, :, t0:t0 + T], writes=[qbk])
                nkt = NCX // 128 if bi == 0 else NKT
                for n in range(2):
                    op_, opk = ps[4 + n], 'ps%d' % (4 + n)
                    dp_, dpk = ps[6 + n], 'ps%d' % (6 + n)
                    for kt in range(nkt):
                        sp_, spk = next_ps(0, 4)
                        s.i('pe', 'matmul', sp_[:, 0:T], kth[n * 64:(n + 1) * 64, kt * 128:(kt + 1) * 128], qb[n * 64:(n + 1) * 64, 0:T],
                            start=True, stop=True, reads=['kth', qbk], writes=[spk])
                        b = cnt['wk'] % 2; cnt['wk'] += 1
                        pb = wkb[2 + b]; pbk = 'wkb%d' % (2 + b)
                        s.i('act', 'activation', out=pb[:, 0:T], in_=sp_[:, 0:T], func=AF.Exp, scale=0.125, reads=[spk], writes=[pbk])
                        s.i('pe', 'matmul', op_[:, 0:T], vth[:, kt * 128:(kt + 1) * 128], pb[:, 0:T], start=(kt == 0), stop=(kt == nkt - 1),
                            reads=['vth', pbk], writes=[opk])
                        s.i('pe', 'matmul', dp_[:, 0:T], ones_bf[:], pb[:, 0:T], start=(kt == 0), stop=(kt == nkt - 1),
                            reads=['ones', pbk], writes=[dpk])
                s.i('dve', 'reciprocal', wk[0][:, 0:T], ps[6][:, 0:T], reads=['ps6'], writes=['wk0'])
                s.i('dve', 'reciprocal', wk[1][:, 0:T], ps[7][:, 0:T], reads=['ps7'], writes=['wk1'])
                s.i('dve', 'tensor_tensor', wk[0][:, 0:T], wk[0][:, 0:T], ps[4][:, 0:T], ALU.mult, reads=['wk0', 'ps4'], writes=['wk0'])
                s.i('dve', 'tensor_tensor', wk[1][:, 0:T], wk[1][:, 0:T], ps[5][:, 0:T], ALU.mult, reads=['wk1', 'ps5'], writes=['wk1'])
                s.i('dve', 'scalar_tensor_tensor', wk[2][:, 0:T], wk[1][:, 0:T], nlam[:, 0:1], wk[0][:, 0:T], ALU.mult, ALU.add,
                    reads=['wk0', 'wk1', 'nlam'], writes=['wk2'])
                s.i('act', 'activation', out=wk[3][:, 0:T], in_=wk[2][:, 0:T], func=AF.Square, reads=['wk2'], writes=['wk3'])
                sp_, spk = next_ps(0, 4)
                s.i('pe', 'matmul', sp_[:, 0:T], ones_f[:], wk[3][:, 0:T], start=True, stop=True, reads=['ones_f', 'wk3'], writes=[spk])
                s.i('act', 'activation', out=wk[4][:, 0:T], in_=sp_[:, 0:T], func=AF.Sqrt, scale=1.0 / 128, bias=epsc[:, 0:1], reads=[spk], writes=['wk4'])
                s.i('dve', 'reciprocal', wk[4][:, 0:T], wk[4][:, 0:T], reads=['wk4'], writes=['wk4'])
                s.i('dve', 'tensor_tensor', wk[5][:, 0:T], wk[2][:, 0:T], wk[4][:, 0:T], ALU.mult, reads=['wk2', 'wk4'], writes=['wk5'])
                ob = wkb[4 + bi % 2]; obk = 'wkb%d' % (4 + bi % 2)
                s.i('act', 'activation', out=ob[:, 0:T], in_=wk[5][:, 0:T], func=AF.Identity, scale=subg[:, 0:1],
                    reads=['wk5', 'subg'], writes=[obk])
                s.ld(YT[CC + SC + hh, :, t0:t0 + T], ob[:, 0:T], reads=[obk])
        s.barrier()

        for bi, (t0, T) in enumerate(blocks):
            w = 1 if bi == 0 else 0
            for kc in range(MC):
                s.ld(aT[:, kc, 0:T], YT[kc, :, t0:t0 + T], writes=['aT'])
            for oc0 in range(0, DC, 2):
                sl, sk = load_slab(wb_out, 0, MC, oc0 * 128, 256)
                for o in range(2):
                    oc = oc0 + o
                    p, pk = next_ps(0, 6)
                    for kc in range(MC):
                        s.i('pe', 'matmul', p[:, 0:T], sl[:, kc, o * 128:(o + 1) * 128], aT[:, kc, 0:T], start=(kc == 0), stop=(kc == MC - 1),
                            reads=[sk, 'aT'], writes=[pk])
                    s.i('act', 'activation', out=yacc[:, oc, 0:T], in_=p[:, 0:T], func=AF.Copy, reads=[pk], writes=['yacc%d' % oc])
            stats_rstd(lambda c: (yacc[:, c, 0:T], 'yacc%d' % c), DC, T, float(D))
            for dc in range(DC):
                b = dc % 2
                s.ld(wk[b][:, 0:T], XT[dc, :, t0:t0 + T], writes=['wk%d' % b])
                s.i('dve', 'tensor_tensor', yacc[:, dc, 0:T], yacc[:, dc, 0:T], rstd[:, 0:T], ALU.mult, reads=['yacc%d' % dc, 'rstd'], writes=['yacc%d' % dc])
                s.i('dve', 'scalar_tensor_tensor', yacc[:, dc, 0:T], yacc[:, dc, 0:T], gg1[:, dc, w:w + 1], wk[b][:, 0:T], ALU.mult, ALU.add,
                    reads=['yacc%d' % dc, 'gsx', 'wk%d' % b], writes=['yacc%d' % dc])
                s.ld(XT[dc, :, t0:t0 + T], yacc[:, dc, 0:T], reads=['yacc%d' % dc], writes=['XT%d' % dc])
            stats_rstd(lambda c: (yacc[:, c, 0:T], 'yacc%d' % c), DC, T, float(D))
            for dc in range(DC):
                b = dc % 2
                s.i('dve', 'tensor_tensor', wk[b][:, 0:T], yacc[:, dc, 0:T], rstd[:, 0:T], ALU.mult, reads=['yacc%d' % dc, 'rstd'], writes=['wk%d' % b])
                s.i('act', 'activation', out=hT[:, dc, 0:T], in_=wk[b][:, 0:T], func=AF.Identity,
                    scale=gs2[:, dc, w:w + 1], bias=mT[:, 3 * DC + dc, w:w + 1], reads=['wk%d' % b, 'gsx', 'mT'], writes=['hT'])
            for q in range(NQ):
                for f0 in range(0, QFC, 2):
                    sl, sk = load_slab(wb_1, 0, DC, q * QF + f0 * 128, 256)
                    for o in range(2):
                        p, pk = next_ps(0, 6)
                        for dc in range(DC):
                            s.i('pe', 'matmul', p[:, 0:T], sl[:, dc, o * 128:(o + 1) * 128], hT[:, dc, 0:T], start=(dc == 0), stop=(dc == DC - 1),
                                reads=[sk, 'hT'], writes=[pk])
                        b = (f0 + o) % 2
                        s.i('act', 'activation', out=wk[2 + b][:, 0:T], in_=p[:, 0:T], func=AF.Relu, reads=[pk], writes=['wk%d' % (2 + b)])
                        s.i('pool', 'tensor_tensor', aT[:, f0 + o, 0:T], wk[2 + b][:, 0:T], wk[2 + b][:, 0:T], ALU.mult,
                            reads=['wk%d' % (2 + b)], writes=['aT'])
                for oc0 in range(0, DC, 2):
                    sl, sk = load_slab(wb_2, q * QF, QFC, oc0 * 128, 256)
                    for o in range(2):
                        oc = oc0 + o
                        p, pk = next_ps(0, 6)
                        for fc in range(QFC):
                            s.i('pe', 'matmul', p[:, 0:T], sl[:, fc, o * 128:(o + 1) * 128], aT[:, fc, 0:T], start=(fc == 0), stop=(fc == QFC - 1),
                                reads=[sk, 'aT'], writes=[pk])
                        if q == 0:
                            s.i('act', 'activation', out=yacc[:, oc, 0:T], in_=p[:, 0:T], func=AF.Copy, reads=[pk], writes=['yacc%d' % oc])
                        else:
                            s.i('dve', 'tensor_tensor', yacc[:, oc, 0:T], yacc[:, oc, 0:T], p[:, 0:T], ALU.add, reads=['yacc%d' % oc, pk], writes=['yacc%d' % oc])
            stats_rstd(lambda c: (yacc[:, c, 0:T], 'yacc%d' % c), DC, T, float(D))
            last = False
            for dc in range(DC):
                b = dc % 2
                s.ld(wk[b][:, 0:T], XT[dc, :, t0:t0 + T], reads=['XT%d' % dc], writes=['wk%d' % b])
                s.i('dve', 'tensor_tensor', yacc[:, dc, 0:T], yacc[:, dc, 0:T], rstd[:, 0:T], ALU.mult, reads=['yacc%d' % dc, 'rstd'], writes=['yacc%d' % dc])
                s.i('dve', 'scalar_tensor_tensor', yacc[:, dc, 0:T], yacc[:, dc, 0:T], gg2[:, dc, w:w + 1], wk[b][:, 0:T], ALU.mult, ALU.add,
                    reads=['yacc%d' % dc, 'gsx', 'wk%d' % b], writes=['yacc%d' % dc])
                if last:
                    if bi > 0:
                        s.ld(outT[dc, :, t0 - NCX:t0 - NCX + T], yacc[:, dc, 0:T], reads=['yacc%d' % dc])
                else:
                    s.ld(XT[dc, :, t0:t0 + T], yacc[:, dc, 0:T], reads=['yacc%d' % dc])
        s.barrier()

    s.emit()
    es.close()
    return nc


def prep_inputs(inp, cfg):
    D = cfg['D']; NL = cfg['NL']; NCX = cfg['NCX']; L = cfg['L']; GW = cfg['GW']
    DC = D // 128; N = NL + NCX; CW = D // 4; CC = CW // 128; SW = D // 4; SC = SW // 128; G = SW // 16
    f = lambda a: np.ascontiguousarray(np.asarray(a, dtype=np.float32))
    m = {}
    xall = np.concatenate([np.asarray(inp['ctx'])[0], np.asarray(inp['x'])[0]], axis=0)
    m['xT'] = f(xall.T.reshape(DC, 128, N))
    cc = np.stack([np.asarray(inp['c'])[0], np.asarray(inp['c_ctx'])], axis=-1)
    m['cT'] = f(cc.reshape(DC, 128, 2).transpose(1, 0, 2))
    for k in ('mod_down', 'mod_up', 'w_in', 'conv_pw', 'ssm_glu_w', 'w_out', 'mlp_w1', 'mlp_w2'):
        m[k] = f(inp[k])
    m['mod_bT'] = f(np.asarray(inp['mod_b']).reshape(L, 6 * DC, 128).transpose(0, 2, 1))
    m['norm_gT'] = f(np.asarray(inp['norm_g']).reshape(L, 4 * DC, 128).transpose(0, 2, 1))
    m['conv_dwT'] = f(np.asarray(inp['conv_dw']).reshape(L, CK, CC, 128).transpose(0, 3, 2, 1).reshape(L, 128, CC * CK))
    cv = np.stack([np.asarray(inp[k]) for k in ('conv_dw_b', 'conv_ln_g', 'conv_ln_b', 'conv_pw_b')], axis=1)
    m['convv'] = f(cv.reshape(L, 4, CC, 128).transpose(0, 3, 1, 2).reshape(L, 128, 4 * CC))

    def Flay(a):
        a = np.asarray(a).reshape(L, 2, SC, 8, 1, 64)
        a = np.broadcast_to(a, (L, 2, SC, 8, 16, 64))
        return f(a.transpose(0, 1, 3, 4, 2, 5).reshape(L, 2, 128, SC * 64))
    m['areF'] = Flay(inp['ssm_a_re']); m['aimF'] = Flay(inp['ssm_a_im'])
    m['lsF'] = Flay(np.broadcast_to(np.asarray(inp['ssm_log_step'])[..., None], (L, 2, G, 64)))

    def Blay(b):
        b = np.asarray(b).reshape(L, 2, SC, 8, 64, 16)
        return f(b.transpose(0, 1, 3, 5, 2, 4).reshape(L, 2, 128, SC * 64))
    m['breF'] = Blay(inp['ssm_b_re']); m['bimF'] = Blay(inp['ssm_b_im'])
    cre = np.asarray(inp['ssm_c_re']).reshape(L, 2, SC, 8, 16, 64)
    cim = np.asarray(inp['ssm_c_im']).reshape(L, 2, SC, 8, 16, 64)
    cre = cre.transpose(0, 1, 5, 2, 3, 4).reshape(L, 2, 64, SC * 128)
    cim = cim.transpose(0, 1, 5, 2, 3, 4).reshape(L, 2, 64, SC * 128)
    m['cS'] = f(np.concatenate([cre, cim], axis=2))

    def Slay(a):
        a = np.asarray(a).transpose(0, 1, 3, 2)
        return f(np.concatenate([a, a], axis=2))
    m['areS'] = Slay(inp['ssm_a_re']); m['aimS'] = Slay(inp['ssm_a_im'])
    m['lsS'] = f(np.broadcast_to(np.asarray(inp['ssm_log_step'])[:, :, None, :], (L, 2, 128, G)))
    sv = np.stack([np.asarray(inp['ssm_d']), np.asarray(inp['ssm_glu_b'])], axis=1)
    m['ssmv'] = f(sv.reshape(L, 2, SC, 128).transpose(0, 3, 1, 2).reshape(L, 128, 2 * SC))
    m['lamv'] = f(np.stack([np.asarray(inp[k]) for k in ('lam_q1', 'lam_k1', 'lam_q2', 'lam_k2')], axis=1))
    m['sublnT'] = f(np.asarray(inp['attn_subln_g']).reshape(L, 128, 1))
    tok = np.arange(NL)
    row = (tok // GW).astype(np.float32); col = (tok % GW).astype(np.float32)
    pos = np.zeros((128, N), np.float32)
    for p in range(128):
        pos[p, NCX:] = row if (p % 64) < 32 else col
    m['posT'] = pos
    m['jfreq'] = (np.arange(128) % 16).astype(np.float32).reshape(128, 1)
    rm = np.zeros((128, 8), np.float32)
    for r in range(128):
        rm[r, r // 16] = 1.0
    m['rowmask'] = rm
    m['iota512'] = f(np.broadcast_to(np.arange(512, dtype=np.float32)[None], (128, 512)))
    R = np.zeros((128, 128), np.float32)
    for p in range(128):
        if (p % 32) < 16:
            R[p + 16, p] = -1.0
        else:
            R[p - 16, p] = 1.0
    m['ropeR'] = R
    return m


LAYER_KEYS = ('mod_down', 'mod_up', 'mod_bT', 'norm_gT', 'w_in', 'conv_pw', 'ssm_glu_w', 'w_out', 'mlp_w1', 'mlp_w2',
              'conv_dwT', 'convv', 'areF', 'aimF', 'lsF', 'breF', 'bimF', 'cS', 'areS', 'aimS', 'lsS', 'ssmv', 'lamv', 'sublnT')


def run(inp, cfg, debug=False):
    L = cfg['L']
    cfg1 = dict(cfg); cfg1['L'] = 1
    nc = build(cfg1, debug)
    m = prep_inputs(inp, cfg)
    xT = m['xT']
    r = None
    for l in range(L):
        ml = {k: v for k, v in m.items() if k not in LAYER_KEYS}
        for k in LAYER_KEYS:
            ml[k] = np.ascontiguousarray(m[k][l:l + 1])
        ml['xT'] = xT
        li = 0.8 - 0.6 * math.exp(-0.3 * l)
        ml['laminit'] = np.tile(np.array([[li, 1.0 - li]], np.float32), (128, 1))
        res = run_bass_kernel_spmd(nc, [ml], core_ids=[0])
        r = res.results[0]
        xT = np.ascontiguousarray(np.asarray(r['XT'], dtype=np.float32))
    D = cfg['D']; NL = cfg['NL']; NCX = cfg['NCX']
    out = xT.reshape(D, NL + NCX)[:, NCX:].T[None]
    return np.ascontiguousarray(out.astype(np.float32)), r


def kernel(**inputs):
    out, _ = run(inputs, FULL)
    return out
```
